# Optimizing a Trainium2 kernel written in Bass

```python
import jax, jax.numpy as jnp
from jax import lax
import numpy as np

D_MODEL = 1024
BATCH = 2
SEQ = 8192
DEPTH = 1

N_HEADS = 8
HEAD_DIM = 64
ATTN_WIDTH = N_HEADS * HEAD_DIM
N_IDX_HEADS = 8
IDX_DIM = 64
TOPK_MAX = 256
Q_BLOCK = 128
CONV_CH = 512
CONV_KERNEL = 31
D_FF = -(-8 * D_MODEL // (3 * 256)) * 256
PLE_DIM = 256
EPS = 1e-6

IN_SIZES = (ATTN_WIDTH, ATTN_WIDTH, ATTN_WIDTH, N_IDX_HEADS * IDX_DIM, IDX_DIM,
            N_IDX_HEADS, 2 * CONV_CH, D_MODEL, D_MODEL)
IN_WIDTH = sum(IN_SIZES)

kernel_name = "hybrid_dsa_conformer_gated_block"


def _split_points():
    pts, acc = [], 0
    for s in IN_SIZES[:-1]:
        acc += s
        pts.append(acc)
    return pts


def rms_norm(x, g):
    xf = x.astype(jnp.float32)
    y = xf * lax.rsqrt(jnp.mean(xf * xf, axis=-1, keepdims=True) + EPS)
    return (y * g.astype(jnp.float32)).astype(x.dtype)


def layer_norm(x, g, b):
    xf = x.astype(jnp.float32)
    mu = jnp.mean(xf, axis=-1, keepdims=True)
    var = jnp.mean(jnp.square(xf - mu), axis=-1, keepdims=True)
    y = (xf - mu) * lax.rsqrt(var + EPS)
    return (y * g.astype(jnp.float32) + b.astype(jnp.float32)).astype(x.dtype)


def dsa_attention(q, k, v, qi, ki, wi, topk):
    B, S, H, D = q.shape
    nb = S // Q_BLOCK
    key_pos = jnp.arange(S)
    q = q * (D ** -0.5)
    wi = wi * (N_IDX_HEADS ** -0.5)

    def block(n):
        t0 = n * Q_BLOCK
        qb = lax.dynamic_slice_in_dim(q, t0, Q_BLOCK, axis=1)
        qib = lax.dynamic_slice_in_dim(qi, t0, Q_BLOCK, axis=1)
        wib = lax.dynamic_slice_in_dim(wi, t0, Q_BLOCK, axis=1)
        qpos = t0 + jnp.arange(Q_BLOCK)
        causal = key_pos[None, :] <= qpos[:, None]
        rel = jax.nn.relu(jnp.einsum('bqhd,bsd->bqhs', qib, ki).astype(jnp.float32)
                          * (IDX_DIM ** -0.5))
        score = jnp.einsum('bqhs,bqh->bqs', rel, wib.astype(jnp.float32))
        score = jnp.where(causal[None], score, -jnp.inf)
        _, idx = lax.top_k(score, topk)
        ksel = jax.vmap(lambda kb, ib: kb[ib])(k, idx)
        vsel = jax.vmap(lambda vb, ib: vb[ib])(v, idx)
        logits = jnp.einsum('bqhd,bqkhd->bqhk', qb, ksel).astype(jnp.float32)
        valid = idx <= qpos[None, :, None]
        logits = jnp.where(valid[:, :, None, :], logits, -jnp.inf)
        probs = jax.nn.softmax(logits, axis=-1).astype(v.dtype)
        return jnp.einsum('bqhk,bqkhd->bqhd', probs, vsel)

    out = lax.map(block, jnp.arange(nb))
    return out.transpose(1, 0, 2, 3, 4).reshape(B, S, H * D)


def conformer_conv(u, conv_w, conv_b, ln_g, ln_b):
    a, g = jnp.split(u, 2, axis=-1)
    glu = a * jax.nn.sigmoid(g)
    y = lax.conv_general_dilated(
        glu, conv_w.astype(glu.dtype), window_strides=(1,),
        padding=[(CONV_KERNEL - 1, 0)],
        dimension_numbers=('NWC', 'WIO', 'NWC'),
        feature_group_count=CONV_CH)
    y = y + conv_b
    y = layer_norm(y, ln_g, ln_b)
    return jax.nn.silu(y)


def setup_inputs(seed: int = 0) -> dict:
    key = jax.random.key(seed)
    ks = jax.random.split(key, 24)
    f32 = jnp.float32

    def nrm(k, shape, fan_in):
        return jax.random.normal(k, shape, f32) * (fan_in ** -0.5)

    def gain(k, shape):
        return 1.0 + 0.01 * jax.random.normal(k, shape, f32)

    L = DEPTH
    return {
        "x": jax.random.normal(ks[0], (BATCH, SEQ, D_MODEL), f32),
        "p": jax.random.normal(ks[1], (DEPTH, BATCH, SEQ, PLE_DIM), f32),
        "g_mix": gain(ks[2], (L, D_MODEL)),
        "w_in": nrm(ks[3], (L, D_MODEL, IN_WIDTH), D_MODEL),
        "g_q": gain(ks[4], (L, HEAD_DIM)),
        "g_k": gain(ks[5], (L, HEAD_DIM)),
        "conv_w": nrm(ks[6], (L, CONV_KERNEL, 1, CONV_CH), CONV_KERNEL),
        "conv_b": 0.01 * jax.random.normal(ks[7], (L, CONV_CH), f32),
        "conv_ln_g": gain(ks[8], (L, CONV_CH)),
        "conv_ln_b": 0.01 * jax.random.normal(ks[9], (L, CONV_CH), f32),
        "w_attn_o": nrm(ks[10], (L, ATTN_WIDTH, D_MODEL), ATTN_WIDTH),
        "w_conv_o": nrm(ks[11], (L, CONV_CH, D_MODEL), CONV_CH),
        "w_out": nrm(ks[12], (L, D_MODEL, D_MODEL), D_MODEL),
        "g_ffn": gain(ks[13], (L, D_MODEL)),
        "w_ffn_gate": nrm(ks[14], (L, D_MODEL, D_FF), D_MODEL),
        "w_ffn_up": nrm(ks[15], (L, D_MODEL, D_FF), D_MODEL),
        "w_ffn_down": nrm(ks[16], (L, D_FF, D_MODEL), D_FF),
        "g_ple": gain(ks[17], (L, D_MODEL)),
        "w_ple_gate": nrm(ks[18], (L, D_MODEL, D_MODEL), D_MODEL),
        "w_ple_proj": nrm(ks[19], (L, PLE_DIM, D_MODEL), PLE_DIM),
    }


def reference(x, p, g_mix, w_in, g_q, g_k, conv_w, conv_b, conv_ln_g, conv_ln_b,
              w_attn_o, w_conv_o, w_out, g_ffn, w_ffn_gate, w_ffn_up, w_ffn_down,
              g_ple, w_ple_gate, w_ple_proj):
    B, S, _ = x.shape
    topk = min(TOPK_MAX, S // 4)
    pts = _split_points()
    for i in range(DEPTH):
        h = rms_norm(x, g_mix[i])
        proj = h @ w_in[i]
        q, k, v, qi, ki, wi, conv_in, gate_a, gate_c = jnp.split(proj, pts, axis=-1)
        q = rms_norm(q.reshape(B, S, N_HEADS, HEAD_DIM), g_q[i])
        k = rms_norm(k.reshape(B, S, N_HEADS, HEAD_DIM), g_k[i])
        v = v.reshape(B, S, N_HEADS, HEAD_DIM)
        qi = qi.reshape(B, S, N_IDX_HEADS, IDX_DIM)
        attn = dsa_attention(q, k, v, qi, ki, wi, topk)
        conv = conformer_conv(conv_in, conv_w[i], conv_b[i],
                              conv_ln_g[i], conv_ln_b[i])
        merged = (jax.nn.sigmoid(gate_a) * (attn @ w_attn_o[i])
                  + jax.nn.sigmoid(gate_c) * (conv @ w_conv_o[i]))
        x = x + merged @ w_out[i]
        hf = rms_norm(x, g_ffn[i])
        x = x + (jax.nn.silu(hf @ w_ffn_gate[i]) * (hf @ w_ffn_up[i])) @ w_ffn_down[i]
        hp = rms_norm(x, g_ple[i])
        x = x + jax.nn.sigmoid(hp @ w_ple_gate[i]) * (p[i] @ w_ple_proj[i])
    return x
```

```python
import contextlib
import numpy as np
import concourse.bass as bass
import concourse.mybir as mybir
from concourse.bass_utils import run_bass_kernel_spmd

F32 = mybir.dt.float32
BF16 = mybir.dt.bfloat16
ALU = mybir.AluOpType
AF = mybir.ActivationFunctionType
AX = mybir.AxisListType

D = 1024
KC = 8
DFF = 2816
NFF = 22
TOPK = 256
EPS = 1e-6
NITER = 15
ACT_SHARE = 0.5
NEG = -1.0e30


class _Op:
    __slots__ = ("eng", "fn", "deps", "needed", "is_dma", "sem", "val")

    def __init__(self, eng, fn, deps, is_dma):
        self.eng = eng
        self.fn = fn
        self.deps = deps
        self.needed = False
        self.is_dma = is_dma
        self.sem = None
        self.val = None


class Sched:
    COMPUTE = ("pe", "act", "dve", "pool")
    N_DMA_SEMS = 32

    def __init__(self, nc):
        self.nc = nc
        self.ops = {k: [] for k in ("pe", "act", "dve", "pool", "sp")}
        self.last_w = {}
        self.readers = {}
        self.bar_pending = {}
        self.dma_since_bar = []

    def barrier(self):
        deps = []
        for k in self.ops:
            for o in reversed(self.ops[k]):
                if not o.is_dma:
                    deps.append(o)
                    break
        deps.extend(self.dma_since_bar)
        self.dma_since_bar = []
        for d in deps:
            d.needed = True
        self.bar_pending = {k: list(deps) for k in self.ops}

    def op(self, eng, fn, *args, reads=(), writes=(), dma=False, **kw):
        if isinstance(fn, str):
            meth = fn
            fn = lambda e: getattr(e, meth)(*args, **kw)
        deps = self.bar_pending.pop(eng, [])
        for r in reads:
            w = self.last_w.get(r)
            if w is not None:
                deps.append(w)
        for w_ in writes:
            w = self.last_w.get(w_)
            if w is not None:
                deps.append(w)
            deps.extend(self.readers.get(w_, ()))
        o = _Op(eng, fn, deps, dma)
        for d in deps:
            d.needed = True
        self.ops[eng].append(o)
        if dma:
            self.dma_since_bar.append(o)
        for r in reads:
            self.readers.setdefault(r, []).append(o)
        for w_ in writes:
            self.last_w[w_] = o
            self.readers[w_] = []
        return o

    def dma(self, eng, out, in_, reads=(), writes=()):
        return self.op(eng, "dma_start", out=out, in_=in_, reads=reads, writes=writes, dma=True)

    def emit(self, final_ops):
        nc = self.nc
        for o in final_ops:
            o.needed = True
        with contextlib.ExitStack() as st:
            sems = {k: st.enter_context(nc.semaphore("s_" + k)) for k in self.COMPUTE}
            dsems = [st.enter_context(nc.semaphore("d%d" % i)) for i in range(self.N_DMA_SEMS)]
            block = st.enter_context(nc.Block())
            for k in self.COMPUTE:
                c = 0
                for o in self.ops[k]:
                    if o.is_dma:
                        continue
                    if o.needed:
                        c += 1
                        o.sem, o.val = sems[k], c
            dtot = [0] * self.N_DMA_SEMS
            dma_prev = {}
            pools = {"pool": list(range(0, 8)), "sp": list(range(8, self.N_DMA_SEMS))}
            for k in self.ops:
                rr = 0
                for o in self.ops[k]:
                    if o.is_dma:
                        pl = pools[k]
                        i = pl[rr % len(pl)]
                        rr += 1
                        dma_prev[id(o)] = (dsems[i], dtot[i]) if dtot[i] > 0 else None
                        dtot[i] += 16
                        o.sem, o.val = dsems[i], dtot[i]
            handles = {"pe": "tensor", "act": "scalar", "dve": "vector", "pool": "gpsimd", "sp": "sync"}

            def run(k, e):
                seen = {}
                for o in self.ops[k]:
                    waits = {}
                    for d in o.deps:
                        if d.eng == "pe" and k == "pe" and not d.is_dma:
                            continue
                        key = id(d.sem)
                        if seen.get(key, 0) >= d.val:
                            continue
                        if key not in waits or waits[key][1] < d.val:
                            waits[key] = (d.sem, d.val)
                    if o.is_dma:
                        p = dma_prev[id(o)]
                        if p is not None and seen.get(id(p[0]), 0) < p[1]:
                            key = id(p[0])
                            if key not in waits or waits[key][1] < p[1]:
                                waits[key] = p
                    for key, (s, v) in waits.items():
                        e.wait_ge(s, v)
                        seen[key] = v
                    ins = o.fn(e)
                    if o.is_dma:
                        ins.then_inc(o.sem, 16)
                    elif o.needed:
                        ins.then_inc(o.sem, 1)
                if k == "sp":
                    for o in final_ops:
                        e.wait_ge(o.sem, o.val)

            for k in ("pe", "act", "dve", "pool", "sp"):
                getattr(block, handles[k])(lambda e, k=k: run(k, e))


C_Q, C_K, C_V, C_QI, C_KI, C_WI, C_CA, C_CG, C_GA, C_GC = (
    0, 512, 1024, 1536, 2048, 2112, 2120, 2632, 3144, 4168)
IN_W = 5192


def build(S, debug=False):
    NB = S // 512
    NG = S // 512
    NT = S // 128
    NGO = NB // 4
    TOWN = NB * 128

    nc = bass.Bass("TRN2", target_bir_lowering=False)

    def din(name, shape):
        return nc.dram_tensor(name, shape, F32, kind="ExternalInput").ap()

    xb = din("xb", [S, D])
    xo = din("xo", [NB, 160, D])
    po = din("po", [TOWN, 256])
    qpos_d = din("qpos", [128, NB])
    g_mix = din("g_mix", [1, D])
    w_in = din("w_in", [D, IN_W])
    g_q = din("g_q", [1, 64])
    g_k = din("g_k", [1, 64])
    conv_w = din("conv_w", [31, 512])
    conv_b = din("conv_b", [1, 512])
    ln_g = din("conv_ln_g", [1, 512])
    ln_b = din("conv_ln_b", [1, 512])
    w_ao = din("w_attn_o", [512, D])
    w_co = din("w_conv_o", [512, D])
    w_out = din("w_out", [D, D])
    g_ffn = din("g_ffn", [1, D])
    w_fg = din("w_ffn_gate", [D, DFF])
    w_fu = din("w_ffn_up", [D, DFF])
    w_fd = din("w_ffn_down", [DFF, D])
    g_ple = din("g_ple", [1, D])
    w_pg = din("w_ple_gate", [D, D])
    w_pp = din("w_ple_proj", [256, D])
    out_d = nc.dram_tensor("out", [TOWN, D], F32, kind="ExternalOutput").ap()

    vscr = nc.dram_tensor("vscr", [NT, 128, 8 * 65], BF16, kind="Internal").ap()
    ascr = nc.dram_tensor("ascr", [NB, 64, 8 * 128], BF16, kind="ExternalOutput" if debug else "Internal").ap()
    x1scr = nc.dram_tensor("x1scr", [NGO, 128, KC * 512], F32, kind="Internal").ap()
    dbg = None
    if debug:
        dbg = nc.dram_tensor("dbg", [NB, 128, 8], F32, kind="ExternalOutput").ap()

    w_in_r = w_in.rearrange("(kc p) c -> p kc c", p=128)

    S_ = Sched(nc)
    op = S_.op
    dma = S_.dma
    final_ops = []

    with contextlib.ExitStack() as g0:
        def sb(name, shape, dt, st=g0):
            return st.enter_context(nc.sbuf_tensor(name, shape, dt))

        PS = g0.enter_context(nc.psum_tensor("PS", [128, 4096], F32))
        PSB = PS[:].bitcast(BF16)

        def bank(b, n=1):
            return PS[:, b * 512:(b + n) * 512]

        def bank_bf(b):
            return PSB[:, b * 1024:(b + 1) * 1024]

        def pk(b, n=1):
            return [("ps", b + i) for i in range(n)]

        io_f = sb("io_f", [128, 128], F32)
        ident_b = sb("ident_b", [128, 128], BF16)
        ident_f = sb("ident_f", [128, 128], F32)
        kidx = sb("kidx", [128, 512], F32)
        ones_b = sb("ones_b", [128, 128], BF16)
        ones_f = sb("ones_f", [128, 128], F32)
        ones_a = sb("ones_a", [128, 128], BF16)
        zeros_b = sb("zeros_b", [128, 65], BF16)
        gq2 = sb("gq2", [128, 1], F32)
        gk2 = sb("gk2", [128, 1], F32)
        gq_rep = sb("gq_rep", [128, 64], F32)
        gk_rep = sb("gk_rep", [128, 64], F32)
        mq = sb("mq", [128, 1], F32)
        mk = sb("mk", [128, 1], F32)
        negshift = sb("negshift", [128, 1], F32)
        qpos = sb("qpos_sb", [128, NB], F32)
        qrel = sb("qrel", [128, NB], F32)
        c0t = sb("c0t", [128, NB], F32)
        pow2 = sb("pow2", [128, NITER + 1], F32)
        pow2x2 = sb("pow2x2", [128, NITER + 1], F32)
        wi_s = sb("wi_s", [128, NB, 8], F32)

        op("pool", "iota", io_f[:], pattern=[[1, 128]], base=0, channel_multiplier=-1,
                                    allow_small_or_imprecise_dtypes=True, writes=["io_f"])
        op("pool", "iota", kidx[:], pattern=[[1, 512]], base=0, channel_multiplier=0,
                                    allow_small_or_imprecise_dtypes=True, writes=["kidx"])
        op("dve", "tensor_scalar", ident_b[:], io_f[:], 0.0, None, ALU.is_equal,
           reads=["io_f"], writes=["ident_b"])
        op("dve", "tensor_scalar", ident_f[:], io_f[:], 0.0, None, ALU.is_equal,
           reads=["io_f"], writes=["ident_f"])
        op("dve", "memset", ones_b[:], 0.0, writes=["ones_b"])
        op("dve", "memset", ones_b[0:64, 0:64], 1.0, writes=["ones_b"])
        op("dve", "memset", ones_b[64:128, 64:128], 1.0, writes=["ones_b"])
        op("dve", "memset", ones_f[:], 1.0, writes=["ones_f"])
        op("dve", "memset", ones_a[:], 1.0, writes=["ones_a"])
        op("dve", "memset", zeros_b[:], 0.0, writes=["zeros_b"])
        for n in range(NITER + 1):
            op("dve", "memset", pow2[:, n:n + 1], 2.0 ** -(n + 1), writes=["pow2"])
            op("dve", "memset", pow2x2[:, n:n + 1], 2.0 ** -n, writes=["pow2x2"])
        dma("sp", gq_rep[:], g_q.partition_broadcast(128), writes=["gq_rep"])
        dma("sp", gk_rep[:], g_k.partition_broadcast(128), writes=["gk_rep"])
        dma("sp", gq2[0:64, :], g_q.rearrange("o d -> d o"), writes=["gq2"])
        dma("sp", gq2[64:128, :], g_q.rearrange("o d -> d o"), writes=["gq2"])
        dma("sp", gk2[0:64, :], g_k.rearrange("o d -> d o"), writes=["gk2"])
        dma("sp", gk2[64:128, :], g_k.rearrange("o d -> d o"), writes=["gk2"])
        dma("sp", qpos[:], qpos_d, writes=["qpos"])
        op("dve", "tensor_scalar", gq2[:], gq2[:], 0.125, None, ALU.mult, reads=["gq2"], writes=["gq2"])
        op("dve", "tensor_reduce", mq[:], gq_rep[:], AX.X, ALU.max, apply_absolute_value=True,
           reads=["gq_rep"], writes=["mq"])
        op("dve", "tensor_reduce", mk[:], gk_rep[:], AX.X, ALU.max, apply_absolute_value=True,
           reads=["gk_rep"], writes=["mk"])
        op("dve", "scalar_tensor_tensor", negshift[:], mq[:], -8.0, mk[:], ALU.mult, ALU.mult,
           reads=["mq", "mk"], writes=["negshift"])
        for i in range(NB):
            op("dve", "tensor_scalar", qrel[:, i:i + 1], qpos[:, i:i + 1], float(-512 * i), None, ALU.add,
               reads=["qpos"], writes=["qrel"])
        op("dve", "tensor_scalar", kidx[:], kidx[:], -2.0e30, None, ALU.mult, reads=["kidx"], writes=["kidx"])
        op("dve", "tensor_scalar", c0t[:], qrel[:], 2.0e30, 1.0e30, ALU.mult, ALU.add, reads=["qrel"], writes=["c0t"])

        st4 = [sb("st4_%d" % i, [128, 4], F32) for i in range(3)]
        nb_ = {}

        def alloc_norm(st, tag):
            nb_["xt"] = [sb("xt%s%d" % (tag, i), [128, D], F32, st) for i in range(3)]
            nb_["xs"] = [sb("xs%s%d" % (tag, i), [128, D], BF16, st) for i in range(2)]
            nb_["junk"] = sb("junk" + tag, [128, D], BF16, st)
            nb_["gmix"] = sb("gmix" + tag, [128, D], F32, st)
            dma("sp", nb_["gmix"][:], g_mix.partition_broadcast(128), writes=["gmix_rep"])

        def norm_stream(tiles, after=None):
            N = len(tiles)
            junk, gmix_rep = nb_["junk"], nb_["gmix"]

            def s0(n):
                src, P, dest, keys = tiles[n]
                dma("sp", nb_["xt"][n % 3][0:P, :], src, writes=[("xt", n % 3)])

            def s1(n):
                src, P, dest, keys = tiles[n]
                X, XS, ST = nb_["xt"][n % 3], nb_["xs"][n % 2], st4[n % 3]
                op("act", "activation", junk[0:P, :], X[0:P, :], AF.Square, accum_out=ST[0:P, 0:1],
                   reads=[("xt", n % 3)], writes=[("st", n % 3, 0)])
                op("act", "activation", ST[0:P, 1:2], ST[0:P, 0:1], AF.Ln, bias=EPS, scale=1.0 / D,
                   reads=[("st", n % 3, 0)], writes=[("st", n % 3, 1)])
                op("act", "activation", ST[0:P, 2:3], ST[0:P, 1:2], AF.Exp, scale=-0.5,
                   reads=[("st", n % 3, 1)], writes=[("st", n % 3, 2)])
                op("dve", "scalar_tensor_tensor", XS[0:P, :], X[0:P, :], ST[0:P, 2:3], gmix_rep[0:P, :],
                   ALU.mult, ALU.mult, reads=[("xt", n % 3), ("st", n % 3, 2), "gmix_rep"], writes=[("xs", n % 2)])

            def s2(n):
                src, P, dest, keys = tiles[n]
                XS = nb_["xs"][n % 2]
                tbank = n % 2
                tp = bank_bf(tbank)
                for kc in range(KC):
                    op("pe", "transpose", tp[:, kc * 128:kc * 128 + P], XS[0:P, kc * 128:(kc + 1) * 128],
                       ident_b[0:P, 0:P], reads=[("xs", n % 2), "ident_b"], writes=pk(tbank))
                tpv = tp.rearrange("p (k t) -> p k t", k=KC)[:, :, 0:P]
                op("act", "activation", dest, tpv, AF.Copy, reads=pk(tbank), writes=keys)

            s0(0)
            if N > 1:
                s0(1)
            s1(0)
            for n in range(N):
                if n + 2 < N:
                    s0(n + 2)
                if n + 1 < N:
                    s1(n + 1)
                s2(n)
                if after and n in after:
                    after[n]()

        with contextlib.ExitStack() as gA:
            KT = sb("KT", [128, 4, S], BF16, gA)
            kiT = sb("kiT", [128, S], BF16, gA)
            qT = sb("qT", [128, 4, TOWN], BF16, gA)
            qiT = sb("qiT", [128, 4, TOWN], BF16, gA)

            with contextlib.ExitStack() as gA1:
                alloc_norm(gA1, "A")
                WK = sb("WK", [128, KC, 512], BF16, gA1)
                WV = sb("WV", [128, KC, 512], BF16, gA1)
                WKI = sb("WKI", [128, KC, 128], BF16, gA1)
                hT = [sb("hT%d" % i, [128, KC, 512], BF16, gA1) for i in range(2)]
                sq = [sb("sq%d" % i, [128, 512], BF16, gA1) for i in range(2)]
                sd = [sb("sd%d" % i, [128, 512], F32, gA1) for i in range(2)]
                vst = [sb("vst%d" % i, [128, 8, 65], BF16, gA1) for i in range(2)]
                dma("pool", WK[:], w_in_r[:, :, C_K:C_K + 512], writes=["WK"])
                dma("pool", WV[:], w_in_r[:, :, C_V:C_V + 512], writes=["WV"])
                dma("pool", WKI[:, :, 0:64], w_in_r[:, :, C_KI:C_KI + 64], writes=["WKI"])
                dma("pool", WKI[:, :, 64:128], w_in_r[:, :, C_KI:C_KI + 64], writes=["WKI"])
                for i in range(2):
                    op("dve", "memset", vst[i][:, :, 64:65], 1.0, writes=[("vst", i)])
                def groupA(G):
                    hs = G % 2
                    H = hT[hs]

                    def Kmm(pr):
                        kb_ = 2 + (pr % 2)
                        s2 = pr % 2
                        for kc in range(KC):
                            op("pe", "matmul", bank(kb_), WK[:, kc, pr * 128:(pr + 1) * 128], H[:, kc, :],
                               start=(kc == 0), stop=(kc == KC - 1), reads=["WK", ("hT", hs)], writes=pk(kb_))
                        op("act", "activation", sq[s2][:], bank(kb_), AF.Square, reads=pk(kb_), writes=[("sq", s2)])

                    def Kones(pr):
                        kb_ = 2 + (pr % 2)
                        s2 = pr % 2
                        op("pe", "matmul", bank(4), ones_b[:], sq[s2][:], start=True, stop=True,
                           reads=["ones_b", ("sq", s2)], writes=pk(4))
                        op("act", "activation", sd[s2][:], bank(4), AF.Ln, bias=EPS, scale=1.0 / 64,
                           reads=pk(4), writes=[("sd", s2)])
                        op("act", "activation", sd[s2][:], sd[s2][:], AF.Exp, scale=-0.5,
                           reads=[("sd", s2)], writes=[("sd", s2)])
                        op("dve", "scalar_tensor_tensor", KT[:, pr, G * 512:(G + 1) * 512], bank(kb_), gk2[:, 0:1],
                           sd[s2][:], ALU.mult, ALU.mult, reads=pk(kb_) + [("sd", s2), "gk2"], writes=["KT"])

                    def KI():
                        for kc in range(KC):
                            op("pe", "matmul", bank(5), WKI[:, kc, :], H[:, kc, :], start=(kc == 0), stop=(kc == KC - 1),
                               reads=["WKI", ("hT", hs)], writes=pk(5))
                        op("act", "activation", kiT[:, G * 512:(G + 1) * 512], bank(5), AF.Copy,
                           reads=pk(5), writes=["kiT"])

                    def Vt(tt):
                        T = 4 * G + tt
                        vb = 6 + (tt % 2)
                        vs_ = tt % 2
                        for kc in range(KC):
                            op("pe", "matmul", bank(vb), H[:, kc, tt * 128:(tt + 1) * 128], WV[:, kc, :],
                               start=(kc == 0), stop=(kc == KC - 1), reads=["WV", ("hT", hs)], writes=pk(vb))
                        op("dve", "tensor_copy", vst[vs_][:, :, 0:64], bank(vb).rearrange("p (h d) -> p h d", h=8),
                           reads=pk(vb), writes=[("vst", vs_)])
                        dma("sp", vscr[T], vst[vs_][:].rearrange("p h d -> p (h d)"),
                            reads=[("vst", vs_)], writes=[("vscr", T)])

                    Kmm(0)
                    Kmm(1)
                    KI()
                    Vt(0)
                    Kones(0)
                    Vt(1)
                    Kones(1)
                    Kmm(2)
                    Vt(2)
                    Kmm(3)
                    Vt(3)
                    Kones(2)
                    Kones(3)

                tilesA = []
                for G in range(NG):
                    for tt in range(4):
                        T = 4 * G + tt
                        tilesA.append((xb[T * 128:(T + 1) * 128, :], 128, hT[G % 2][:, :, tt * 128:(tt + 1) * 128],
                                       [("hT", G % 2)]))
                norm_stream(tilesA, {4 * G + 3: (lambda G=G: groupA(G)) for G in range(NG)})

            S_.barrier()
            with contextlib.ExitStack() as gB:
                alloc_norm(gB, "B")
                WQ = sb("WQ", [128, KC, 512], BF16, gB)
                WQI = sb("WQI", [128, KC, 512], BF16, gB)
                WWI = sb("WWI", [128, KC, 8], BF16, gB)
                hTo = [sb("hTo%d" % i, [128, KC, 512], BF16, gB) for i in range(2)]
                sqb = [sb("sqb%d" % i, [128, 512], BF16, gB) for i in range(2)]
                sdb = [sb("sdb%d" % i, [128, 512], F32, gB) for i in range(2)]
                dma("pool", WQ[:], w_in_r[:, :, C_Q:C_Q + 512], writes=["WQ"])
                dma("pool", WQI[:], w_in_r[:, :, C_QI:C_QI + 512], writes=["WQI"])
                dma("pool", WWI[:], w_in_r[:, :, C_WI:C_WI + 8], writes=["WWI"])
                def groupB(g):
                    hs = g % 2
                    H = hTo[hs]
                    for pr in range(4):
                        kb_ = 2 + (pr % 2)
                        s2 = pr % 2
                        for kc in range(KC):
                            op("pe", "matmul",
                                bank(kb_), WQ[:, kc, pr * 128:(pr + 1) * 128], H[:, kc, :],
                                start=(kc == 0), stop=(kc == KC - 1),
                               reads=["WQ", ("hTo", hs)], writes=pk(kb_))
                        op("act", "activation", sqb[s2][:], bank(kb_), AF.Square,
                           reads=pk(kb_), writes=[("sqb", s2)])
                        op("pe", "matmul", bank(4), ones_b[:], sqb[s2][:], start=True, stop=True,
                           reads=["ones_b", ("sqb", s2)], writes=pk(4))
                        op("act", "activation", sdb[s2][:], bank(4), AF.Ln, bias=EPS, scale=1.0 / 64,
                           reads=pk(4), writes=[("sdb", s2)])
                        op("act", "activation", sdb[s2][:], sdb[s2][:], AF.Exp, scale=-0.5,
                           reads=[("sdb", s2)], writes=[("sdb", s2)])
                        op("dve", "scalar_tensor_tensor",
                            qT[:, pr, g * 512:(g + 1) * 512], bank(kb_), gq2[:, 0:1], sdb[s2][:], ALU.mult, ALU.mult,
                           reads=pk(kb_) + [("sdb", s2), "gq2"], writes=["qT"])
                    for pr in range(4):
                        kb_ = 6 + (pr % 2)
                        for kc in range(KC):
                            op("pe", "matmul",
                                bank(kb_), WQI[:, kc, pr * 128:(pr + 1) * 128], H[:, kc, :],
                                start=(kc == 0), stop=(kc == KC - 1),
                               reads=["WQI", ("hTo", hs)], writes=pk(kb_))
                        op("act", "activation",
                            qiT[:, pr, g * 512:(g + 1) * 512], bank(kb_), AF.Copy,
                           reads=pk(kb_), writes=["qiT"])
                    for bb in range(4):
                        i = 4 * g + bb
                        for kc in range(KC):
                            op("pe", "matmul",
                                bank(5)[:, 0:8], H[:, kc, bb * 128:(bb + 1) * 128], WWI[:, kc, :],
                                start=(kc == 0), stop=(kc == KC - 1),
                               reads=["WWI", ("hTo", hs)], writes=pk(5))
                        op("dve", "tensor_scalar", wi_s[:, i, :], bank(5)[:, 0:8], 8.0 ** -0.5, None, ALU.mult,
                           reads=pk(5), writes=["wi_s"])

                tilesB = []
                for g in range(NGO):
                    for bb in range(4):
                        i = 4 * g + bb
                        tilesB.append((xo[i, 32:160, :], 128, hTo[g % 2][:, :, bb * 128:(bb + 1) * 128], [("hTo", g % 2)]))
                norm_stream(tilesB, {4 * g + 3: (lambda g=g: groupB(g)) for g in range(NGO)})

            S_.barrier()
            with contextlib.ExitStack() as gC:
                score = sb("score", [128, S], F32, gC)
                maskq = sb("maskq", [128, S], BF16, gC)
                maskT = sb("maskT", [128, NT, 128], BF16, gC)
                rbuf = [sb("rbuf%d" % i, [128, 2, 512], BF16, gC) for i in range(3)]
                wdiag = sb("wdiag", [128, 8, 128], BF16, gC)
                gmax = sb("gmax", [128, 256], F32, gC)
                pen_t = rbuf[0][:].rearrange("p a b -> p (a b)").bitcast(F32)
                bs = sb("bs", [128, 8], F32, gC)
                Wt = sb("Wt", [128, NITER + 1], F32, gC)
                Wt2 = sb("Wt2", [128, NITER + 1], F32, gC)
                vbuf = [sb("vbuf%d" % i, [128, 8 * 65], BF16, gC) for i in range(4)]
                pT = [sb("pT%d" % i, [128, 8, 128], BF16, gC) for i in range(3)]
                att = [sb("att%d" % i, [64, 1024], BF16, gC) for i in range(2)]
                op("dve", "memset", bs[:], 0.0, writes=["bs0", "bs1", "bs2", "mid", "cnt", "tmp", "thr", "cntA"])
                vc_ = {"n": 0}

                def indexer(i):
                    E = 512 * (i + 1)
                    tsl = slice(i * 128, (i + 1) * 128)
                    for h in range(8):
                        op("dve", "tensor_scalar", wdiag[:, h, :], ident_b[:], wi_s[:, i, h:h + 1], None, ALU.mult,
                           reads=["ident_b", "wi_s"], writes=["wdiag"])
                    pairs = [(c, h2) for c in range(i + 1) for h2 in range(4)]

                    def zmm(n):
                        c, h2 = pairs[n]
                        zb = 2 * (n % 3)
                        for hh in range(2):
                            rows = slice(64 * hh, 64 * hh + 64)
                            op("pe", "matmul", bank(zb + hh), qiT[rows, h2, tsl], kiT[rows, c * 512:(c + 1) * 512],
                               start=True, stop=True, reads=["qiT", "kiT"], writes=pk(zb + hh))

                    pend = []
                    pend2 = []

                    def post(c_):
                        sc = score[:, c_ * 512:(c_ + 1) * 512]
                        if c_ == i:
                            op("dve", "scalar_tensor_tensor", sc, kidx[:], c0t[:, i:i + 1], sc, ALU.add, ALU.min,
                               reads=["kidx", "c0t", "score"], writes=["score"])
                        lo_, hi_ = sc[:, 0:256], sc[:, 256:512]
                        if c_ == 0:
                            op("dve", "tensor_tensor", gmax[:], lo_, hi_, ALU.max, reads=["score"], writes=["gmax"])
                        else:
                            op("dve", "tensor_tensor", gmax[:], gmax[:], lo_, ALU.max, reads=["score", "gmax"], writes=["gmax"])
                            op("dve", "tensor_tensor", gmax[:], gmax[:], hi_, ALU.max, reads=["score", "gmax"], writes=["gmax"])

                    zmm(0)
                    if len(pairs) > 1:
                        zmm(1)
                    for n, (c, h2) in enumerate(pairs):
                        sbk = 6 if c % 2 == 0 else 7
                        zb = 2 * (n % 3)
                        rs = n % 3
                        if n + 2 < len(pairs):
                            zmm(n + 2)
                        rflat = rbuf[rs][:].rearrange("p a b -> p (a b)")
                        if n % 2 == 0:
                            op("act", "activation", rflat, bank(zb, 2), AF.Relu, scale=0.125,
                               reads=pk(zb, 2), writes=[("rbuf", rs)])
                        else:
                            op("dve", "tensor_scalar", rflat, bank(zb, 2), 0.0, 0.125, ALU.max, ALU.mult,
                               reads=pk(zb, 2), writes=[("rbuf", rs)])
                        for hh in range(2):
                            h = 2 * h2 + hh
                            op("pe", "matmul", bank(sbk), wdiag[:, h, :], rbuf[rs][:, hh, :],
                               start=(h == 0), stop=(h == 7), reads=["wdiag", ("rbuf", rs)], writes=pk(sbk))
                        if pend2 and n >= pend2[0][0] + 2:
                            post(pend2.pop(0)[1])
                        if pend and n >= pend[0][0] + 2:
                            _, c_, sbk_ = pend.pop(0)
                            op("act", "activation", score[:, c_ * 512:(c_ + 1) * 512], bank(sbk_), AF.Copy,
                               reads=pk(sbk_), writes=["score"])
                            pend2.append((n, c_))
                        if h2 == 3:
                            pend.append((n, c, sbk))
                    for _, c_, sbk_ in pend:
                        op("act", "activation", score[:, c_ * 512:(c_ + 1) * 512], bank(sbk_), AF.Copy,
                           reads=pk(sbk_), writes=["score"])
                        pend2.append((0, c_))
                    for _, c_ in pend2:
                        post(c_)
                    op("dve", "tensor_reduce", bs[:, 0:1], gmax[:], AX.X, ALU.min, reads=["gmax"], writes=["bs0"])
                    op("dve", "tensor_reduce", bs[:, 1:2], gmax[:], AX.X, ALU.max, reads=["gmax"], writes=["bs1"])
                    op("dve", "tensor_scalar", bs[:, 0:1], bs[:, 0:1], -1.0e29, -0.05, ALU.max, ALU.add,
                       reads=["bs0"], writes=["bs0"])
                    op("dve", "tensor_tensor", bs[:, 2:3], bs[:, 1:2], bs[:, 0:1], ALU.subtract,
                       reads=["bs0", "bs1"], writes=["bs2"])
                    op("dve", "tensor_scalar", Wt[:], pow2[:], bs[:, 2:3], None, ALU.mult, reads=["pow2", "bs2"], writes=["Wt"])
                    op("dve", "tensor_scalar", Wt2[:], pow2x2[:], bs[:, 2:3], None, ALU.mult,
                       reads=["pow2x2", "bs2"], writes=["Wt2"])
                    op("dve", "tensor_tensor", bs[:, 3:4], bs[:, 0:1], Wt[:, 0:1], ALU.add, reads=["bs0", "Wt"], writes=["mid"])

                def bisect_iter(i, n):
                    E = 512 * (i + 1)
                    EA = (int(E * ACT_SHARE) // 512) * 512
                    if EA > 0:
                        op("act", "activation", maskq[:, 0:EA], score[:, 0:EA], AF.Sign, bias=bs[:, 3:4], scale=-1.0,
                           accum_out=bs[:, 7:8], reads=["score", "mid"], writes=["maskqA", "cntA"])
                    op("dve", "tensor_scalar", maskq[:, EA:E], score[:, EA:E], bs[:, 3:4], None, ALU.is_gt, ALU.add,
                       accum_out=bs[:, 4:5], reads=["score", "mid"], writes=["maskqD", "cnt"])
                    if EA > 0:
                        op("dve", "tensor_scalar", bs[:, 4:5], bs[:, 7:8], -0.5, bs[:, 4:5], ALU.mult, ALU.add,
                           reads=["cntA", "cnt"], writes=["cnt"])
                    op("dve", "tensor_scalar", bs[:, 5:6], bs[:, 4:5], TOPK - 0.5 - EA / 2.0, Wt2[:, n + 1:n + 2],
                       ALU.is_ge, ALU.mult, reads=["cnt", "Wt2"], writes=["tmp"])
                    op("dve", "tensor_scalar", bs[:, 3:4], bs[:, 5:6], Wt[:, n + 1:n + 2], bs[:, 3:4],
                       ALU.subtract, ALU.add, reads=["tmp", "Wt", "mid"], writes=["mid"])

                def finish_mask(i):
                    E = 512 * (i + 1)
                    nkb = 4 * (i + 1)
                    op("dve", "tensor_tensor", bs[:, 6:7], bs[:, 3:4], Wt[:, NITER:NITER + 1], ALU.subtract,
                       reads=["mid", "Wt"], writes=["thr"])
                    for k8 in range(0, nkb, 8):
                        nn = min(8, nkb - k8)
                        cols = slice(k8 * 128, (k8 + nn) * 128)
                        op("dve", "tensor_scalar", maskq[:, cols], score[:, cols], bs[:, 6:7], None, ALU.is_gt,
                           reads=["score", "thr"], writes=[("mq", k8)])
                    if debug:
                        final_ops.append(dma("sp", dbg[i], bs[:], reads=["thr", "cnt", "mid", "bs0", "bs1", "bs2", "tmp", "cntA"]))
                    for k8 in range(0, nkb, 8):
                        nn = min(8, nkb - k8)
                        tb = 6 if (k8 // 8) % 2 == 0 else 7
                        for kk in range(nn):
                            kb = k8 + kk
                            op("pe", "transpose", bank_bf(tb)[:, kk * 128:(kk + 1) * 128], maskq[:, kb * 128:(kb + 1) * 128],
                               ident_b[:], reads=[("mq", k8), "ident_b"], writes=pk(tb))
                        if (k8 // 8) % 4 != 3:
                            op("act", "activation", maskT[:, k8:k8 + nn, :].rearrange("p a b -> p (a b)"),
                               bank_bf(tb)[:, 0:nn * 128], AF.Copy, reads=pk(tb), writes=["maskT"])
                        else:
                            op("dve", "tensor_copy", maskT[:, k8:k8 + nn, :].rearrange("p a b -> p (a b)"),
                               bank_bf(tb)[:, 0:nn * 128], reads=pk(tb), writes=["maskT"])

                def attention(i):
                    nkb = 4 * (i + 1)
                    tsl = slice(i * 128, (i + 1) * 128)
                    for b2 in (6, 7):
                        op("pe", "matmul", bank(b2)[0:65, :], zeros_b[:, :], KT[:, 0, 0:512], start=True, stop=True,
                           reads=["zeros_b", "KT"], writes=pk(b2))

                    def qk(kb):
                        lb = 2 * (kb % 3)
                        for pr in range(4):
                            for half in range(2):
                                rows = slice(64 * half, 64 * half + 64)
                                op("pe", "matmul", bank(lb + half)[:, pr * 128:(pr + 1) * 128],
                                   KT[rows, pr, kb * 128:(kb + 1) * 128], qT[rows, pr, tsl], start=True, stop=True,
                                   reads=["KT", "qT"], writes=pk(lb + half))

                    def vload(kb):
                        dma("sp", vbuf[kb % 4][:], vscr[kb], reads=[("vscr", kb)], writes=[("vbuf", kb % 4)])

                    vload(0)
                    vload(1)
                    qk(0)
                    if nkb > 1:
                        qk(1)
                    for kb in range(nkb):
                        vs_ = kb % 4
                        if kb + 2 < nkb:
                            vload(kb + 2)
                        lb = 2 * (kb % 3)
                        ps_ = kb % 3
                        op("act", "activation", pT[ps_][:].rearrange("p a b -> p (a b)"), bank(lb, 2), AF.Exp,
                           bias=negshift[:, 0:1], reads=pk(lb, 2) + ["negshift"], writes=[("pT", ps_)])
                        if kb + 2 < nkb:
                            qk(kb + 2)
                        op("pool", "tensor_tensor", pT[ps_][:], pT[ps_][:],
                           maskT[:, kb:kb + 1, :].broadcast_to([128, 8, 128]), ALU.mult,
                           reads=[("pT", ps_), "maskT"], writes=[("pT", ps_)])
                        for h in range(8):
                            b2 = 6 + h // 4
                            hs_ = (h % 2) * 4 + h // 2
                            op("pe", "matmul", bank(b2)[0:65, (h % 4) * 128:(h % 4 + 1) * 128],
                               vbuf[vs_][:, h * 65:(h + 1) * 65], pT[ps_][:, hs_, :], start=False, stop=(kb == nkb - 1),
                               skip_group_check=True, reads=[("vbuf", vs_), ("pT", ps_)], writes=pk(b2))
                        yield
                    as_ = i % 2
                    rdh = wdiag[:].rearrange("p a b -> p (a b)").bitcast(F32)[0:64, :]
                    for hb in range(2):
                        op("act", "activation", pen_t[64:65, :], bank(6 + hb)[64:65, :], AF.Copy,
                           reads=pk(6 + hb), writes=[("rbuf", 0)])
                        op("pe", "matmul", bank(hb)[0:64, :], ones_f[64:65, 0:64], pen_t[64:65, :],
                           start=True, stop=True, reads=["ones_f", ("rbuf", 0)], writes=pk(hb))
                        op("act", "activation", rdh, bank(hb)[0:64, :], AF.Ln, reads=pk(hb), writes=["wdiag"])
                        op("act", "activation", rdh, rdh, AF.Exp, scale=-1.0, reads=["wdiag"], writes=["wdiag"])
                        op("dve", "tensor_tensor", att[as_][:, hb * 512:(hb + 1) * 512], bank(6 + hb)[0:64, :], rdh, ALU.mult,
                           reads=pk(6 + hb) + ["wdiag"], writes=[("att", as_)])
                    o_ = dma("sp", ascr[i], att[as_][:], reads=[("att", as_)], writes=[("ascr", i)])
                    if debug:
                        final_ops.append(o_)
                    yield

                def advance(gen, k):
                    if gen is None:
                        return None
                    for _ in range(k):
                        try:
                            next(gen)
                        except StopIteration:
                            return None
                    return gen

                agen = None
                for i in range(NB):
                    indexer(i)
                    per = -(-(4 * i + 1) // NITER) if i > 0 else 0
                    for n in range(NITER):
                        bisect_iter(i, n)
                        agen = advance(agen, per)
                    agen = advance(agen, 10 ** 6)
                    finish_mask(i)
                    agen = attention(i)
                advance(agen, 10 ** 6)

        S_.barrier()
        def col_load(dst, src_row, n):
            for c in range(n):
                dma("sp", dst[:, c:c + 1], src_row[0:1, c * 128:(c + 1) * 128].rearrange("o p -> p o"),
                    writes=[("col", dst.name)])

        with contextlib.ExitStack() as gDE:
            hT_own = sb("hT_own", [128, KC, TOWN], BF16, gDE)
            convT = sb("convT", [128, 4, TOWN], BF16, gDE)
            with contextlib.ExitStack() as gD:
                alloc_norm(gD, "D")
                WCA = sb("WCA", [128, KC, 512], BF16, gD)
                WCG = sb("WCG", [128, KC, 512], BF16, gD)
                diag = sb("diag", [128, 31, 4, 128], BF16, gD)
                cw_sb = sb("cw_sb", [31, 512], F32, gD)
                cwT = sb("cwT", [128, 4, 31], F32, gD)
                cb_c = sb("cb_c", [128, 4], F32, gD)
                lng_c = sb("lng_c", [128, 4], F32, gD)
                lnb_c = sb("lnb_c", [128, 4], F32, gD)
                hT_gh = [sb("hT_gh%d" % i, [128, KC, 4, 160], BF16, gD) for i in range(2)]
                sgb = [sb("sgb%d" % i, [128, 320], F32, gD) for i in range(2)]
                glu = sb("glu", [128, 4, 4, 160], BF16, gD)
                yb = sb("yb", [128, 4, 512], F32, gD)
                sqy = sb("sqy", [128, 4, 512], F32, gD)
                sdv = sb("sdv", [128, 512], F32, gD)
                dma("pool", WCA[:], w_in_r[:, :, C_CA:C_CA + 512], writes=["WCA"])
                dma("pool", WCG[:], w_in_r[:, :, C_CG:C_CG + 512], writes=["WCG"])
                dma("sp", cw_sb[:], conv_w, writes=["cw_sb"])
                col_load(cb_c, conv_b, 4)
                col_load(lng_c, ln_g, 4)
                col_load(lnb_c, ln_b, 4)
                for cc in range(4):
                    op("pe", "transpose", bank(0)[:, cc * 32:cc * 32 + 31], cw_sb[0:31, cc * 128:(cc + 1) * 128],
                       ident_f[0:31, 0:31], reads=["cw_sb", "ident_f"], writes=pk(0))
                op("dve", "tensor_copy", cwT[:], bank(0)[:, 0:128].rearrange("p (c j) -> p c j", c=4)[:, :, 0:31],
                   reads=pk(0), writes=["cwT"])
                for j in range(31):
                    for cc in range(4):
                        op("dve", "tensor_scalar", diag[:, j, cc, :], ident_b[:], cwT[:, cc, j:j + 1], None, ALU.mult,
                           reads=["ident_b", "cwT"], writes=["diag"])
                def groupD(g):
                    gs = g % 2
                    HG = hT_gh[gs]
                    for bb in range(4):
                        i = 4 * g + bb
                        op("pool", "tensor_copy", hT_own[:, :, i * 128:(i + 1) * 128], HG[:, :, bb, 32:160],
                           reads=[("hT_gh", gs)], writes=["hT_own"])
                    n_ = 0
                    for half in range(2):
                        for cc in range(4):
                            ab, gb = 2 + n_ % 2, 4 + n_ % 2
                            ss_ = n_ % 2
                            n_ += 1
                            for kc in range(KC):
                                op("pe", "matmul", bank(ab)[:, 0:320], WCA[:, kc, cc * 128:(cc + 1) * 128],
                                   HG[:, kc, 2 * half:2 * half + 2, :], start=(kc == 0), stop=(kc == KC - 1),
                                   reads=["WCA", ("hT_gh", gs)], writes=pk(ab))
                            for kc in range(KC):
                                op("pe", "matmul", bank(gb)[:, 0:320], WCG[:, kc, cc * 128:(cc + 1) * 128],
                                   HG[:, kc, 2 * half:2 * half + 2, :], start=(kc == 0), stop=(kc == KC - 1),
                                   reads=["WCG", ("hT_gh", gs)], writes=pk(gb))
                            op("act", "activation", sgb[ss_][:], bank(gb)[:, 0:320], AF.Sigmoid,
                               reads=pk(gb), writes=[("sgb", ss_)])
                            op("dve", "tensor_tensor", glu[:, cc, 2 * half:2 * half + 2, :].rearrange("p a b -> p (a b)"),
                               bank(ab)[:, 0:320], sgb[ss_][:], ALU.mult,
                               reads=pk(ab) + [("sgb", ss_)], writes=["glu"])
                    for cc in range(4):
                        cb_ = 6 + cc % 2
                        for j in range(31):
                            op("pe", "matmul", bank(cb_), diag[:, j, cc, :], glu[:, cc, :, 2 + j:2 + j + 128],
                               start=(j == 0), stop=(j == 30), reads=["diag", "glu"], writes=pk(cb_))
                        op("act", "activation", yb[:, cc, :], bank(cb_), AF.Identity, bias=cb_c[:, cc:cc + 1],
                           reads=pk(cb_) + [("col", "cb_c")], writes=["yb"])
                    for cc in range(4):
                        op("pe", "matmul", bank(2), ones_f[:], yb[:, cc, :], start=(cc == 0), stop=(cc == 3),
                           reads=["ones_f", "yb"], writes=pk(2))
                    for cc in range(4):
                        op("dve", "scalar_tensor_tensor", yb[:, cc, :], bank(2), -1.0 / 512, yb[:, cc, :],
                           ALU.mult, ALU.add, reads=pk(2) + ["yb"], writes=["yb"])
                    op("act", "activation", sqy[:].rearrange("p a b -> p (a b)"), yb[:].rearrange("p a b -> p (a b)"),
                       AF.Square, reads=["yb"], writes=["sqy"])
                    for cc in range(4):
                        op("pe", "matmul", bank(3), ones_f[:], sqy[:, cc, :], start=(cc == 0), stop=(cc == 3),
                           reads=["ones_f", "sqy"], writes=pk(3))
                    op("act", "activation", sdv[:], bank(3), AF.Ln, bias=EPS, scale=1.0 / 512,
                       reads=pk(3), writes=["sdv"])
                    op("act", "activation", sdv[:], sdv[:], AF.Exp, scale=-0.5, reads=["sdv"], writes=["sdv"])
                    for cc in range(4):
                        op("dve", "scalar_tensor_tensor", yb[:, cc, :], yb[:, cc, :], lng_c[:, cc:cc + 1], sdv[:],
                           ALU.mult, ALU.mult, reads=["yb", "sdv", ("col", "lng_c")], writes=["yb"])
                        op("act", "activation", convT[:, cc, g * 512:(g + 1) * 512], yb[:, cc, :], AF.Silu,
                           bias=lnb_c[:, cc:cc + 1], reads=["yb", ("col", "lnb_c")], writes=["convT"])

                tilesD = []
                for g in range(NGO):
                    for bb in range(4):
                        i = 4 * g + bb
                        tilesD.append((xo[i, 0:32, :], 32, hT_gh[g % 2][:, :, bb, 0:32], [("hT_gh", g % 2)]))
                        tilesD.append((xo[i, 32:160, :], 128, hT_gh[g % 2][:, :, bb, 32:160], [("hT_gh", g % 2)]))
                norm_stream(tilesD, {8 * g + 7: (lambda g=g: groupD(g)) for g in range(NGO)})
            S_.barrier()
            with contextlib.ExitStack() as gE:
                WGA = sb("WGA", [128, KC, D], BF16, gE)
                WGC = sb("WGC", [128, KC, D], BF16, gE)
                WAO = sb("WAO", [64, 8, D], BF16, gE)
                WCO = sb("WCO", [128, 4, D], BF16, gE)
                WOUT = sb("WOUT", [128, KC, D], BF16, gE)
                attn_g = [sb("attn_g%d" % i, [64, 8, 4, 128], BF16, gE) for i in range(2)]
                xe = [sb("xe%d" % i, [128, D], F32, gE) for i in range(2)]
                x1T = [sb("x1T%d" % i, [128, KC, 512], F32, gE) for i in range(2)]
                sga = [sb("sga%d" % i, [128, 512], F32, gE) for i in range(2)]
                sgc = [sb("sgc%d" % i, [128, 512], F32, gE) for i in range(2)]
                m1 = [sb("m1_%d" % i, [128, 512], F32, gE) for i in range(2)]
                m2 = [sb("m2_%d" % i, [128, 512], F32, gE) for i in range(2)]
                mT = sb("mT", [128, KC, 512], BF16, gE)
                dma("pool", WGA[:], w_in_r[:, :, C_GA:C_GA + D], writes=["WGA"])
                dma("pool", WGC[:], w_in_r[:, :, C_GC:C_GC + D], writes=["WGC"])
                dma("pool", WAO[:], w_ao.rearrange("(h d) c -> d h c", d=64), writes=["WAO"])
                dma("pool", WCO[:], w_co.rearrange("(cc p) c -> p cc c", p=128), writes=["WCO"])
                dma("pool", WOUT[:], w_out.rearrange("(kc p) c -> p kc c", p=128), writes=["WOUT"])
                xn = 0
                for g in range(NGO):
                    gs = g % 2
                    AG = attn_g[gs]
                    X1 = x1T[gs]
                    for bb in range(4):
                        i = 4 * g + bb
                        dma("sp", AG[:, :, bb, :], ascr[i].rearrange("d (h t) -> d h t", h=8),
                            reads=[("ascr", i)], writes=[("attn_g", gs)])
                        xs_ = xn % 2
                        xn += 1
                        dma("sp", xe[xs_][:], xo[i, 32:160, :], writes=[("xe", xs_)])
                        for q4 in range(2):
                            for k4 in range(4):
                                kc = 4 * q4 + k4
                                op("pe", "transpose", bank(q4)[:, k4 * 128:(k4 + 1) * 128],
                                   xe[xs_][:, kc * 128:(kc + 1) * 128], ident_f[:],
                                   reads=[("xe", xs_), "ident_f"], writes=pk(q4))
                            op("act", "activation", X1[:, 4 * q4:4 * q4 + 4, bb * 128:(bb + 1) * 128],
                               bank(q4).rearrange("p (k t) -> p k t", k=4), AF.Copy,
                               reads=pk(q4), writes=[("x1T", gs)])
                    for dc in range(KC):
                        ds_ = dc % 2
                        dsl = slice(dc * 128, (dc + 1) * 128)
                        for h in range(8):
                            op("pe", "matmul", bank(2), WAO[0:64, h, dsl], AG[0:64, h, :, :],
                               start=(h == 0), stop=(h == 7), reads=["WAO", ("attn_g", gs)], writes=pk(2))
                        for cc in range(4):
                            op("pe", "matmul", bank(3), WCO[:, cc, dsl], convT[:, cc, g * 512:(g + 1) * 512],
                               start=(cc == 0), stop=(cc == 3), reads=["WCO", "convT"], writes=pk(3))
                        for kc in range(KC):
                            op("pe", "matmul", bank(4), WGA[:, kc, dsl], hT_own[:, kc, g * 512:(g + 1) * 512],
                               start=(kc == 0), stop=(kc == KC - 1), reads=["WGA", "hT_own"], writes=pk(4))
                        for kc in range(KC):
                            op("pe", "matmul", bank(5), WGC[:, kc, dsl], hT_own[:, kc, g * 512:(g + 1) * 512],
                               start=(kc == 0), stop=(kc == KC - 1), reads=["WGC", "hT_own"], writes=pk(5))
                        op("act", "activation", sga[ds_][:], bank(4), AF.Sigmoid, reads=pk(4), writes=[("sga", ds_)])
                        op("act", "activation", sgc[ds_][:], bank(5), AF.Sigmoid, reads=pk(5), writes=[("sgc", ds_)])
                        op("dve", "tensor_tensor", m1[ds_][:], bank(2), sga[ds_][:], ALU.mult,
                           reads=pk(2) + [("sga", ds_)], writes=[("m1", ds_)])
                        op("dve", "tensor_tensor", m2[ds_][:], bank(3), sgc[ds_][:], ALU.mult,
                           reads=pk(3) + [("sgc", ds_)], writes=[("m2", ds_)])
                        op("dve", "tensor_tensor", mT[:, dc, :], m1[ds_][:], m2[ds_][:], ALU.add,
                           reads=[("m1", ds_), ("m2", ds_)], writes=["mT"])
                    for dc2 in range(KC):
                        yb_ = 6 + dc2 % 2
                        for dc in range(KC):
                            op("pe", "matmul", bank(yb_), WOUT[:, dc, dc2 * 128:(dc2 + 1) * 128], mT[:, dc, :],
                               start=(dc == 0), stop=(dc == KC - 1), reads=["WOUT", "mT"], writes=pk(yb_))
                        op("dve", "tensor_tensor", X1[:, dc2, :], X1[:, dc2, :], bank(yb_), ALU.add,
                           reads=pk(yb_) + [("x1T", gs)], writes=[("x1T", gs)])
                    dma("sp", x1scr[g], X1[:].rearrange("p k t -> p (k t)"), reads=[("x1T", gs)], writes=[("x1scr", g)])
        S_.barrier()
        with contextlib.ExitStack() as gF:
            NH = TOWN // 1024
            xh = sb("xh", [128, KC, 1024], F32, gF)
            sqx = sb("sqx", [128, KC, 512], BF16, gF)
            sdx = sb("sdx", [128, 512], F32, gF)
            hfT = sb("hfT", [128, KC, 1024], BF16, gF)
            actT = sb("actT", [128, NFF, 1024], BF16, gF)
            gffn_c = sb("gffn_c", [128, KC], F32, gF)
            gple_c = sb("gple_c", [128, KC], F32, gF)
            Wg_s = [sb("Wg_s%d" % i, [128, KC, 128], BF16, gF) for i in range(3)]
            Wu_s = [sb("Wu_s%d" % i, [128, KC, 128], BF16, gF) for i in range(3)]
            Wd_s = [sb("Wd_s%d" % i, [128, NFF, 128], BF16, gF) for i in range(2)]
            WPG = sb("WPG", [128, KC, D], BF16, gF)
            WPP = sb("WPP", [128, 2, D], BF16, gF)
            sgl = [sb("sgl%d" % i, [128, 512], F32, gF) for i in range(2)]
            pob = [sb("pob%d" % i, [128, 256], BF16, gF) for i in range(2)]
            ppT = sb("ppT", [128, 2, 1024], BF16, gF)
            otile = [sb("otile%d" % i, [128, D], F32, gF) for i in range(2)]
            col_load(gffn_c, g_ffn, KC)
            col_load(gple_c, g_ple, KC)
            w_fg_r = w_fg.rearrange("(kc p) f -> p kc f", p=128)
            w_fu_r = w_fu.rearrange("(kc p) f -> p kc f", p=128)
            w_fd_r = w_fd.rearrange("(fc p) d -> p fc d", p=128)

            def rms_feat(gcol, dst):
                for tg in range(2):
                    tsl_ = slice(tg * 512, (tg + 1) * 512)
                    op("act", "activation", sqx[:], xh[:, :, tsl_], AF.Square, reads=["xh"], writes=["sqx"])
                    for kc in range(KC):
                        op("pe", "matmul", bank(0), ones_a[:], sqx[:, kc, :], start=(kc == 0), stop=(kc == KC - 1),
                           reads=["ones_a", "sqx"], writes=pk(0))
                    op("act", "activation", sdx[:], bank(0), AF.Ln, bias=EPS, scale=1.0 / D,
                       reads=pk(0), writes=["sdx"])
                    op("act", "activation", sdx[:], sdx[:], AF.Exp, scale=-0.5, reads=["sdx"], writes=["sdx"])
                    for kc in range(KC):
                        op("dve", "scalar_tensor_tensor", dst[:, kc, tsl_], xh[:, kc, tsl_], gcol[:, kc:kc + 1], sdx[:],
                           ALU.mult, ALU.mult, reads=["xh", "sdx", ("col", gcol.name)], writes=["hfT"])

            wn = 0
            dn = 0
            on = 0
            pn = 0
            for hf in range(NH):
                for gi in range(2):
                    g = 2 * hf + gi
                    dma("sp", xh[:, :, gi * 512:(gi + 1) * 512], x1scr[g].rearrange("p (k t) -> p k t", k=KC),
                        reads=[("x1scr", g)], writes=["xh"])
                rms_feat(gffn_c, hfT)
                n_ = 0
                for ffc in range(NFF):
                    ws = wn % 3
                    wn += 1
                    fsl = slice(ffc * 128, (ffc + 1) * 128)
                    dma("pool", Wg_s[ws][:], w_fg_r[:, :, fsl], writes=[("Wg_s", ws)])
                    dma("pool", Wu_s[ws][:], w_fu_r[:, :, fsl], writes=[("Wu_s", ws)])
                    if hf == 0 and ffc == 2:
                        dma("pool", WPG[:], w_pg.rearrange("(kc p) c -> p kc c", p=128), writes=["WPG"])
                        dma("pool", WPP[:], w_pp.rearrange("(kc p) c -> p kc c", p=128), writes=["WPP"])
                    for tg in range(2):
                        tsl_ = slice(tg * 512, (tg + 1) * 512)
                        gb, ub, ss_ = 1 + n_ % 2, 3 + n_ % 2, n_ % 2
                        n_ += 1
                        for kc in range(KC):
                            op("pe", "matmul", bank(gb), Wg_s[ws][:, kc, :], hfT[:, kc, tsl_],
                               start=(kc == 0), stop=(kc == KC - 1), reads=[("Wg_s", ws), "hfT"], writes=pk(gb))
                        for kc in range(KC):
                            op("pe", "matmul", bank(ub), Wu_s[ws][:, kc, :], hfT[:, kc, tsl_],
                               start=(kc == 0), stop=(kc == KC - 1), reads=[("Wu_s", ws), "hfT"], writes=pk(ub))
                        op("act", "activation", sgl[ss_][:], bank(gb), AF.Silu, reads=pk(gb), writes=[("sgl", ss_)])
                        op("dve", "tensor_tensor", actT[:, ffc, tsl_], sgl[ss_][:], bank(ub), ALU.mult,
                           reads=pk(ub) + [("sgl", ss_)], writes=["actT"])
                n_ = 0
                for dc2 in range(KC):
                    ds_ = dn % 2
                    dn += 1
                    dma("pool", Wd_s[ds_][:], w_fd_r[:, :, dc2 * 128:(dc2 + 1) * 128], writes=[("Wd_s", ds_)])
                    for tg in range(2):
                        tsl_ = slice(tg * 512, (tg + 1) * 512)
                        yb_ = 5 + n_ % 2
                        n_ += 1
                        for ffc in range(NFF):
                            op("pe", "matmul", bank(yb_), Wd_s[ds_][:, ffc, :], actT[:, ffc, tsl_],
                               start=(ffc == 0), stop=(ffc == NFF - 1), reads=[("Wd_s", ds_), "actT"], writes=pk(yb_))
                        op("dve", "tensor_tensor", xh[:, dc2, tsl_], xh[:, dc2, tsl_], bank(yb_), ALU.add,
                           reads=pk(yb_) + ["xh"], writes=["xh"])
                rms_feat(gple_c, hfT)
                for tt in range(8):
                    ps_ = pn % 2
                    pn += 1
                    r0 = hf * 1024 + tt * 128
                    dma("pool", pob[ps_][:], po[r0:r0 + 128, :], writes=[("pob", ps_)])
                    for k2 in range(2):
                        op("pe", "transpose", bank_bf(7)[:, k2 * 128:(k2 + 1) * 128], pob[ps_][:, k2 * 128:(k2 + 1) * 128],
                           ident_b[:], reads=[("pob", ps_), "ident_b"], writes=pk(7))
                    op("act", "activation", ppT[:, :, tt * 128:(tt + 1) * 128],
                       bank_bf(7)[:, 0:256].rearrange("p (k t) -> p k t", k=2), AF.Copy,
                       reads=pk(7), writes=["ppT"])
                n_ = 0
                for dc in range(KC):
                    dsl = slice(dc * 128, (dc + 1) * 128)
                    for tg in range(2):
                        tsl_ = slice(tg * 512, (tg + 1) * 512)
                        gb, ub, ss_ = 1 + n_ % 2, 3 + n_ % 2, n_ % 2
                        n_ += 1
                        for kc in range(KC):
                            op("pe", "matmul", bank(gb), WPG[:, kc, dsl], hfT[:, kc, tsl_],
                               start=(kc == 0), stop=(kc == KC - 1), reads=["WPG", "hfT"], writes=pk(gb))
                        for k2 in range(2):
                            op("pe", "matmul", bank(ub), WPP[:, k2, dsl], ppT[:, k2, tsl_],
                               start=(k2 == 0), stop=(k2 == 1), reads=["WPP", "ppT"], writes=pk(ub))
                        op("act", "activation", sgl[ss_][:], bank(gb), AF.Sigmoid, reads=pk(gb), writes=[("sgl", ss_)])
                        op("dve", "tensor_tensor", sgl[ss_][:], sgl[ss_][:], bank(ub), ALU.mult,
                           reads=pk(ub) + [("sgl", ss_)], writes=[("sgl", ss_)])
                        op("dve", "tensor_tensor", xh[:, dc, tsl_], xh[:, dc, tsl_], sgl[ss_][:], ALU.add,
                           reads=["xh", ("sgl", ss_)], writes=["xh"])
                for tt in range(8):
                    os_ = on % 2
                    on += 1
                    r0 = hf * 1024 + tt * 128
                    for q4 in range(2):
                        ob = 5 + q4
                        for k4 in range(4):
                            kc = 4 * q4 + k4
                            op("pe", "transpose", bank(ob)[:, k4 * 128:(k4 + 1) * 128],
                               xh[:, kc, tt * 128:(tt + 1) * 128], ident_f[:],
                               reads=["xh", "ident_f"], writes=pk(ob))
                        op("act", "activation", otile[os_][:, q4 * 512:(q4 + 1) * 512], bank(ob), AF.Copy,
                           reads=pk(ob), writes=[("otile", os_)])
                    final_ops.append(dma("sp", out_d[r0:r0 + 128, :], otile[os_][:],
                                         reads=[("otile", os_)], writes=[("out", r0)]))
        S_.emit(final_ops)
    return nc


def core_inputs(inp, b, j, S):
    NB = S // 512
    x = np.asarray(inp["x"], dtype=np.float32)[b, :S]
    p = np.asarray(inp["p"], dtype=np.float32)[0, b, :S]
    xo = np.zeros((NB, 160, D), np.float32)
    po = np.zeros((NB * 128, 256), np.float32)
    qpos = np.zeros((128, NB), np.float32)
    for i in range(NB):
        t0 = 128 * (4 * i + j)
        lo = max(t0 - 32, 0)
        xo[i, 160 - (t0 + 128 - lo):, :] = x[lo:t0 + 128]
        po[i * 128:(i + 1) * 128] = p[t0:t0 + 128]
        qpos[:, i] = t0 + np.arange(128)
    m = {"xb": np.ascontiguousarray(x), "xo": xo, "po": po, "qpos": qpos}
    for k in ("g_mix", "w_in", "g_q", "g_k", "conv_b", "conv_ln_g", "conv_ln_b", "w_attn_o", "w_conv_o",
              "w_out", "g_ffn", "w_ffn_gate", "w_ffn_up", "w_ffn_down", "g_ple", "w_ple_gate", "w_ple_proj"):
        a = np.asarray(inp[k], dtype=np.float32)[0]
        if a.ndim == 1:
            a = a[None, :]
        m[k] = np.ascontiguousarray(a)
    m["conv_w"] = np.ascontiguousarray(np.asarray(inp["conv_w"], dtype=np.float32)[0, :, 0, :])
    return m


_NC_CACHE = {}


def kernel(**inputs):
    S = int(np.asarray(inputs["x"]).shape[1])
    B = int(np.asarray(inputs["x"]).shape[0])
    if S not in _NC_CACHE:
        _NC_CACHE[S] = build(S)
    nc = _NC_CACHE[S]
    maps = []
    for c in range(4 * B):
        maps.append(core_inputs(inputs, c // 4, c % 4, S))
    res = run_bass_kernel_spmd(nc, maps, core_ids=list(range(4 * B)))
    NB = S // 512
    out = np.zeros((B, S, D), np.float32)
    for c in range(4 * B):
        b, j = c // 4, c % 4
        o = np.asarray(res.results[c]["out"], dtype=np.float32)
        for i in range(NB):
            t0 = 128 * (4 * i + j)
            out[b, t0:t0 + 128] = o[i * 128:(i + 1) * 128]
    return out
```

```python
import contextlib
import numpy as np
import concourse.bass as bass
import concourse.mybir as mybir
from concourse.bass_utils import run_bass_kernel_spmd

F32 = mybir.dt.float32
BF16 = mybir.dt.bfloat16
ALU = mybir.AluOpType
AF = mybir.ActivationFunctionType
AX = mybir.AxisListType

D = 1024
KC = 8
DFF = 2816
NFF = 22
TOPK = 256
EPS = 1e-6
NITER = 15
ACT_SHARE = 0.5
NEG = -1.0e30


class _Op:
    __slots__ = ("eng", "fn", "deps", "needed", "is_dma", "sem", "val")

    def __init__(self, eng, fn, deps, is_dma):
        self.eng = eng
        self.fn = fn
        self.deps = deps
        self.needed = False
        self.is_dma = is_dma
        self.sem = None
        self.val = None


class Sched:
    COMPUTE = ("pe", "act", "dve", "pool")
    N_DMA_SEMS = 32

    def __init__(self, nc):
        self.nc = nc
        self.ops = {k: [] for k in ("pe", "act", "dve", "pool", "sp")}
        self.last_w = {}
        self.readers = {}
        self.bar_pending = {}
        self.dma_since_bar = []

    def barrier(self):
        deps = []
        for k in self.ops:
            for o in reversed(self.ops[k]):
                if not o.is_dma:
                    deps.append(o)
                    break
        deps.extend(self.dma_since_bar)
        self.dma_since_bar = []
        for d in deps:
            d.needed = True
        self.bar_pending = {k: list(deps) for k in self.ops}

    def op(self, eng, fn, *args, reads=(), writes=(), dma=False, **kw):
        if isinstance(fn, str):
            meth = fn
            fn = lambda e: getattr(e, meth)(*args, **kw)
        deps = self.bar_pending.pop(eng, [])
        for r in reads:
            w = self.last_w.get(r)
            if w is not None:
                deps.append(w)
        for w_ in writes:
            w = self.last_w.get(w_)
            if w is not None:
                deps.append(w)
            deps.extend(self.readers.get(w_, ()))
        o = _Op(eng, fn, deps, dma)
        for d in deps:
            d.needed = True
        self.ops[eng].append(o)
        if dma:
            self.dma_since_bar.append(o)
        for r in reads:
            self.readers.setdefault(r, []).append(o)
        for w_ in writes:
            self.last_w[w_] = o
            self.readers[w_] = []
        return o

    def dma(self, eng, out, in_, reads=(), writes=()):
        return self.op(eng, "dma_start", out=out, in_=in_, reads=reads, writes=writes, dma=True)

    def emit(self, final_ops):
        nc = self.nc
        for o in final_ops:
            o.needed = True
        with contextlib.ExitStack() as st:
            sems = {k: st.enter_context(nc.semaphore("s_" + k)) for k in self.COMPUTE}
            dsems = [st.enter_context(nc.semaphore("d%d" % i)) for i in range(self.N_DMA_SEMS)]
            block = st.enter_context(nc.Block())
            for k in self.COMPUTE:
                c = 0
                for o in self.ops[k]:
                    if o.is_dma:
                        continue
                    if o.needed:
                        c += 1
                        o.sem, o.val = sems[k], c
            dtot = [0] * self.N_DMA_SEMS
            dma_prev = {}
            pools = {"pool": list(range(0, 8)), "sp": list(range(8, self.N_DMA_SEMS))}
            for k in self.ops:
                rr = 0
                for o in self.ops[k]:
                    if o.is_dma:
                        pl = pools[k]
                        i = pl[rr % len(pl)]
                        rr += 1
                        dma_prev[id(o)] = (dsems[i], dtot[i]) if dtot[i] > 0 else None
                        dtot[i] += 16
                        o.sem, o.val = dsems[i], dtot[i]
            handles = {"pe": "tensor", "act": "scalar", "dve": "vector", "pool": "gpsimd", "sp": "sync"}

            def run(k, e):
                seen = {}
                for o in self.ops[k]:
                    waits = {}
                    for d in o.deps:
                        if d.eng == "pe" and k == "pe" and not d.is_dma:
                            continue
                        key = id(d.sem)
                        if seen.get(key, 0) >= d.val:
                            continue
                        if key not in waits or waits[key][1] < d.val:
                            waits[key] = (d.sem, d.val)
                    if o.is_dma:
                        p = dma_prev[id(o)]
                        if p is not None and seen.get(id(p[0]), 0) < p[1]:
                            key = id(p[0])
                            if key not in waits or waits[key][1] < p[1]:
                                waits[key] = p
                    for key, (s, v) in waits.items():
                        e.wait_ge(s, v)
                        seen[key] = v
                    ins = o.fn(e)
                    if o.is_dma:
                        ins.then_inc(o.sem, 16)
                    elif o.needed:
                        ins.then_inc(o.sem, 1)
                if k == "sp":
                    for o in final_ops:
                        e.wait_ge(o.sem, o.val)

            for k in ("pe", "act", "dve", "pool", "sp"):
                getattr(block, handles[k])(lambda e, k=k: run(k, e))


C_Q, C_K, C_V, C_QI, C_KI, C_WI, C_CA, C_CG, C_GA, C_GC = (
    0, 512, 1024, 1536, 2048, 2112, 2120, 2632, 3144, 4168)
IN_W = 5192


def build(S, debug=False):
    NB = S // 512
    NG = S // 512
    NT = S // 128
    NGO = NB // 4
    TOWN = NB * 128

    nc = bass.Bass("TRN2", target_bir_lowering=False)

    def din(name, shape):
        return nc.dram_tensor(name, shape, F32, kind="ExternalInput").ap()

    xb = din("xb", [S, D])
    xo = din("xo", [NB, 160, D])
    po = din("po", [TOWN, 256])
    qpos_d = din("qpos", [128, NB])
    g_mix = din("g_mix", [1, D])
    w_in = din("w_in", [D, IN_W])
    g_q = din("g_q", [1, 64])
    g_k = din("g_k", [1, 64])
    conv_w = din("conv_w", [31, 512])
    conv_b = din("conv_b", [1, 512])
    ln_g = din("conv_ln_g", [1, 512])
    ln_b = din("conv_ln_b", [1, 512])
    w_ao = din("w_attn_o", [512, D])
    w_co = din("w_conv_o", [512, D])
    w_out = din("w_out", [D, D])
    g_ffn = din("g_ffn", [1, D])
    w_fg = din("w_ffn_gate", [D, DFF])
    w_fu = din("w_ffn_up", [D, DFF])
    w_fd = din("w_ffn_down", [DFF, D])
    g_ple = din("g_ple", [1, D])
    w_pg = din("w_ple_gate", [D, D])
    w_pp = din("w_ple_proj", [256, D])
    out_d = nc.dram_tensor("out", [TOWN, D], F32, kind="ExternalOutput").ap()

    vscr = nc.dram_tensor("vscr", [NT, 128, 8 * 65], BF16, kind="Internal").ap()
    ascr = nc.dram_tensor("ascr", [NB, 64, 8 * 128], BF16, kind="ExternalOutput" if debug else "Internal").ap()
    x1scr = nc.dram_tensor("x1scr", [NGO, 128, KC * 512], F32, kind="Internal").ap()
    dbg = None
    if debug:
        dbg = nc.dram_tensor("dbg", [NB, 128, 8], F32, kind="ExternalOutput").ap()

    w_in_r = w_in.rearrange("(kc p) c -> p kc c", p=128)

    S_ = Sched(nc)
    op = S_.op
    dma = S_.dma
    final_ops = []

    with contextlib.ExitStack() as g0:
        def sb(name, shape, dt, st=g0):
            return st.enter_context(nc.sbuf_tensor(name, shape, dt))

        PS = g0.enter_context(nc.psum_tensor("PS", [128, 4096], F32))
        PSB = PS[:].bitcast(BF16)

        def bank(b, n=1):
            return PS[:, b * 512:(b + n) * 512]

        def bank_bf(b):
            return PSB[:, b * 1024:(b + 1) * 1024]

        def pk(b, n=1):
            return [("ps", b + i) for i in range(n)]

        io_f = sb("io_f", [128, 128], F32)
        ident_b = sb("ident_b", [128, 128], BF16)
        ident_f = sb("ident_f", [128, 128], F32)
        kidx = sb("kidx", [128, 512], F32)
        ones_b = sb("ones_b", [128, 128], BF16)
        ones_f = sb("ones_f", [128, 128], F32)
        ones_a = sb("ones_a", [128, 128], BF16)
        zeros_b = sb("zeros_b", [128, 65], BF16)
        gq2 = sb("gq2", [128, 1], F32)
        gk2 = sb("gk2", [128, 1], F32)
        gq_rep = sb("gq_rep", [128, 64], F32)
        gk_rep = sb("gk_rep", [128, 64], F32)
        mq = sb("mq", [128, 1], F32)
        mk = sb("mk", [128, 1], F32)
        negshift = sb("negshift", [128, 1], F32)
        qpos = sb("qpos_sb", [128, NB], F32)
        qrel = sb("qrel", [128, NB], F32)
        c0t = sb("c0t", [128, NB], F32)
        pow2 = sb("pow2", [128, NITER + 1], F32)
        pow2x2 = sb("pow2x2", [128, NITER + 1], F32)
        wi_s = sb("wi_s", [128, NB, 8], F32)

        op("pool", "iota", io_f[:], pattern=[[1, 128]], base=0, channel_multiplier=-1,
                                    allow_small_or_imprecise_dtypes=True, writes=["io_f"])
        op("pool", "iota", kidx[:], pattern=[[1, 512]], base=0, channel_multiplier=0,
                                    allow_small_or_imprecise_dtypes=True, writes=["kidx"])
        op("dve", "tensor_scalar", ident_b[:], io_f[:], 0.0, None, ALU.is_equal,
           reads=["io_f"], writes=["ident_b"])
        op("dve", "tensor_scalar", ident_f[:], io_f[:], 0.0, None, ALU.is_equal,
           reads=["io_f"], writes=["ident_f"])
        op("dve", "memset", ones_b[:], 0.0, writes=["ones_b"])
        op("dve", "memset", ones_b[0:64, 0:64], 1.0, writes=["ones_b"])
        op("dve", "memset", ones_b[64:128, 64:128], 1.0, writes=["ones_b"])
        op("dve", "memset", ones_f[:], 1.0, writes=["ones_f"])
        op("dve", "memset", ones_a[:], 1.0, writes=["ones_a"])
        op("dve", "memset", zeros_b[:], 0.0, writes=["zeros_b"])
        for n in range(NITER + 1):
            op("dve", "memset", pow2[:, n:n + 1], 2.0 ** -(n + 1), writes=["pow2"])
            op("dve", "memset", pow2x2[:, n:n + 1], 2.0 ** -n, writes=["pow2x2"])
        dma("sp", gq_rep[:], g_q.partition_broadcast(128), writes=["gq_rep"])
        dma("sp", gk_rep[:], g_k.partition_broadcast(128), writes=["gk_rep"])
        dma("sp", gq2[0:64, :], g_q.rearrange("o d -> d o"), writes=["gq2"])
        dma("sp", gq2[64:128, :], g_q.rearrange("o d -> d o"), writes=["gq2"])
        dma("sp", gk2[0:64, :], g_k.rearrange("o d -> d o"), writes=["gk2"])
        dma("sp", gk2[64:128, :], g_k.rearrange("o d -> d o"), writes=["gk2"])
        dma("sp", qpos[:], qpos_d, writes=["qpos"])
        op("dve", "tensor_scalar", gq2[:], gq2[:], 0.125, None, ALU.mult, reads=["gq2"], writes=["gq2"])
        op("dve", "tensor_reduce", mq[:], gq_rep[:], AX.X, ALU.max, apply_absolute_value=True,
           reads=["gq_rep"], writes=["mq"])
        op("dve", "tensor_reduce", mk[:], gk_rep[:], AX.X, ALU.max, apply_absolute_value=True,
           reads=["gk_rep"], writes=["mk"])
        op("dve", "scalar_tensor_tensor", negshift[:], mq[:], -8.0, mk[:], ALU.mult, ALU.mult,
           reads=["mq", "mk"], writes=["negshift"])
        for i in range(NB):
            op("dve", "tensor_scalar", qrel[:, i:i + 1], qpos[:, i:i + 1], float(-512 * i), None, ALU.add,
               reads=["qpos"], writes=["qrel"])
        op("dve", "tensor_scalar", kidx[:], kidx[:], -2.0e30, None, ALU.mult, reads=["kidx"], writes=["kidx"])
        op("dve", "tensor_scalar", c0t[:], qrel[:], 2.0e30, 1.0e30, ALU.mult, ALU.add, reads=["qrel"], writes=["c0t"])

        st4 = [sb("st4_%d" % i, [128, 4], F32) for i in range(3)]
        nb_ = {}

        def alloc_norm(st, tag):
            nb_["xt"] = [sb("xt%s%d" % (tag, i), [128, D], F32, st) for i in range(3)]
            nb_["xs"] = [sb("xs%s%d" % (tag, i), [128, D], BF16, st) for i in range(2)]
            nb_["junk"] = sb("junk" + tag, [128, D], BF16, st)
            nb_["gmix"] = sb("gmix" + tag, [128, D], F32, st)
            dma("sp", nb_["gmix"][:], g_mix.partition_broadcast(128), writes=["gmix_rep"])

        def norm_stream(tiles, after=None):
            N = len(tiles)
            junk, gmix_rep = nb_["junk"], nb_["gmix"]

            def s0(n):
                src, P, dest, keys = tiles[n]
                dma("sp", nb_["xt"][n % 3][0:P, :], src, writes=[("xt", n % 3)])

            def s1(n):
                src, P, dest, keys = tiles[n]
                X, XS, ST = nb_["xt"][n % 3], nb_["xs"][n % 2], st4[n % 3]
                op("act", "activation", junk[0:P, :], X[0:P, :], AF.Square, accum_out=ST[0:P, 0:1],
                   reads=[("xt", n % 3)], writes=[("st", n % 3, 0)])
                op("act", "activation", ST[0:P, 1:2], ST[0:P, 0:1], AF.Ln, bias=EPS, scale=1.0 / D,
                   reads=[("st", n % 3, 0)], writes=[("st", n % 3, 1)])
                op("act", "activation", ST[0:P, 2:3], ST[0:P, 1:2], AF.Exp, scale=-0.5,
                   reads=[("st", n % 3, 1)], writes=[("st", n % 3, 2)])
                op("dve", "scalar_tensor_tensor", XS[0:P, :], X[0:P, :], ST[0:P, 2:3], gmix_rep[0:P, :],
                   ALU.mult, ALU.mult, reads=[("xt", n % 3), ("st", n % 3, 2), "gmix_rep"], writes=[("xs", n % 2)])

            def s2(n):
                src, P, dest, keys = tiles[n]
                XS = nb_["xs"][n % 2]
                tbank = n % 2
                tp = bank_bf(tbank)
                for kc in range(KC):
                    op("pe", "transpose", tp[:, kc * 128:kc * 128 + P], XS[0:P, kc * 128:(kc + 1) * 128],
                       ident_b[0:P, 0:P], reads=[("xs", n % 2), "ident_b"], writes=pk(tbank))
                tpv = tp.rearrange("p (k t) -> p k t", k=KC)[:, :, 0:P]
                op("act", "activation", dest, tpv, AF.Copy, reads=pk(tbank), writes=keys)

            s0(0)
            if N > 1:
                s0(1)
            s1(0)
            for n in range(N):
                if n + 2 < N:
                    s0(n + 2)
                if n + 1 < N:
                    s1(n + 1)
                s2(n)
                if after and n in after:
                    after[n]()

        with contextlib.ExitStack() as gA:
            KT = sb("KT", [128, 4, S], BF16, gA)
            kiT = sb("kiT", [128, S], BF16, gA)
            qT = sb("qT", [128, 4, TOWN], BF16, gA)
            qiT = sb("qiT", [128, 4, TOWN], BF16, gA)

            with contextlib.ExitStack() as gA1:
                alloc_norm(gA1, "A")
                WK = sb("WK", [128, KC, 512], BF16, gA1)
                WV = sb("WV", [128, KC, 512], BF16, gA1)
                WKI = sb("WKI", [128, KC, 128], BF16, gA1)
                hT = [sb("hT%d" % i, [128, KC, 512], BF16, gA1) for i in range(2)]
                sq = [sb("sq%d" % i, [128, 512], BF16, gA1) for i in range(2)]
                sd = [sb("sd%d" % i, [128, 512], F32, gA1) for i in range(2)]
                vst = [sb("vst%d" % i, [128, 8, 65], BF16, gA1) for i in range(2)]
                dma("pool", WK[:], w_in_r[:, :, C_K:C_K + 512], writes=["WK"])
                dma("pool", WV[:], w_in_r[:, :, C_V:C_V + 512], writes=["WV"])
                dma("pool", WKI[:, :, 0:64], w_in_r[:, :, C_KI:C_KI + 64], writes=["WKI"])
                dma("pool", WKI[:, :, 64:128], w_in_r[:, :, C_KI:C_KI + 64], writes=["WKI"])
                for i in range(2):
                    op("dve", "memset", vst[i][:, :, 64:65], 1.0, writes=[("vst", i)])
                def groupA(G):
                    hs = G % 2
                    H = hT[hs]

                    def Kmm(pr):
                        kb_ = 2 + (pr % 2)
                        s2 = pr % 2
                        for kc in range(KC):
                            op("pe", "matmul", bank(kb_), WK[:, kc, pr * 128:(pr + 1) * 128], H[:, kc, :],
                               start=(kc == 0), stop=(kc == KC - 1), reads=["WK", ("hT", hs)], writes=pk(kb_))
                        op("act", "activation", sq[s2][:], bank(kb_), AF.Square, reads=pk(kb_), writes=[("sq", s2)])

                    def Kones(pr):
                        kb_ = 2 + (pr % 2)
                        s2 = pr % 2
                        op("pe", "matmul", bank(4), ones_b[:], sq[s2][:], start=True, stop=True,
                           reads=["ones_b", ("sq", s2)], writes=pk(4))
                        op("act", "activation", sd[s2][:], bank(4), AF.Ln, bias=EPS, scale=1.0 / 64,
                           reads=pk(4), writes=[("sd", s2)])
                        op("act", "activation", sd[s2][:], sd[s2][:], AF.Exp, scale=-0.5,
                           reads=[("sd", s2)], writes=[("sd", s2)])
                        op("dve", "scalar_tensor_tensor", KT[:, pr, G * 512:(G + 1) * 512], bank(kb_), gk2[:, 0:1],
                           sd[s2][:], ALU.mult, ALU.mult, reads=pk(kb_) + [("sd", s2), "gk2"], writes=["KT"])

                    def KI():
                        for kc in range(KC):
                            op("pe", "matmul", bank(5), WKI[:, kc, :], H[:, kc, :], start=(kc == 0), stop=(kc == KC - 1),
                               reads=["WKI", ("hT", hs)], writes=pk(5))
                        op("act", "activation", kiT[:, G * 512:(G + 1) * 512], bank(5), AF.Copy,
                           reads=pk(5), writes=["kiT"])

                    def Vt(tt):
                        T = 4 * G + tt
                        vb = 6 + (tt % 2)
                        vs_ = tt % 2
                        for kc in range(KC):
                            op("pe", "matmul", bank(vb), H[:, kc, tt * 128:(tt + 1) * 128], WV[:, kc, :],
                               start=(kc == 0), stop=(kc == KC - 1), reads=["WV", ("hT", hs)], writes=pk(vb))
                        op("dve", "tensor_copy", vst[vs_][:, :, 0:64], bank(vb).rearrange("p (h d) -> p h d", h=8),
                           reads=pk(vb), writes=[("vst", vs_)])
                        dma("sp", vscr[T], vst[vs_][:].rearrange("p h d -> p (h d)"),
                            reads=[("vst", vs_)], writes=[("vscr", T)])

                    Kmm(0)
                    Kmm(1)
                    KI()
                    Vt(0)
                    Kones(0)
                    Vt(1)
                    Kones(1)
                    Kmm(2)
                    Vt(2)
                    Kmm(3)
                    Vt(3)
                    Kones(2)
                    Kones(3)

                tilesA = []
                for G in range(NG):
                    for tt in range(4):
                        T = 4 * G + tt
                        tilesA.append((xb[T * 128:(T + 1) * 128, :], 128, hT[G % 2][:, :, tt * 128:(tt + 1) * 128],
                                       [("hT", G % 2)]))
                norm_stream(tilesA, {4 * G + 3: (lambda G=G: groupA(G)) for G in range(NG)})

            S_.barrier()
            with contextlib.ExitStack() as gB:
                alloc_norm(gB, "B")
                WQ = sb("WQ", [128, KC, 512], BF16, gB)
                WQI = sb("WQI", [128, KC, 512], BF16, gB)
                WWI = sb("WWI", [128, KC, 8], BF16, gB)
                hTo = [sb("hTo%d" % i, [128, KC, 512], BF16, gB) for i in range(2)]
                sqb = [sb("sqb%d" % i, [128, 512], BF16, gB) for i in range(2)]
                sdb = [sb("sdb%d" % i, [128, 512], F32, gB) for i in range(2)]
                dma("pool", WQ[:], w_in_r[:, :, C_Q:C_Q + 512], writes=["WQ"])
                dma("pool", WQI[:], w_in_r[:, :, C_QI:C_QI + 512], writes=["WQI"])
                dma("pool", WWI[:], w_in_r[:, :, C_WI:C_WI + 8], writes=["WWI"])
                def groupB(g):
                    hs = g % 2
                    H = hTo[hs]
                    for pr in range(4):
                        kb_ = 2 + (pr % 2)
                        s2 = pr % 2
                        for kc in range(KC):
                            op("pe", "matmul",
                                bank(kb_), WQ[:, kc, pr * 128:(pr + 1) * 128], H[:, kc, :],
                                start=(kc == 0), stop=(kc == KC - 1),
                               reads=["WQ", ("hTo", hs)], writes=pk(kb_))
                        op("act", "activation", sqb[s2][:], bank(kb_), AF.Square,
                           reads=pk(kb_), writes=[("sqb", s2)])
                        op("pe", "matmul", bank(4), ones_b[:], sqb[s2][:], start=True, stop=True,
                           reads=["ones_b", ("sqb", s2)], writes=pk(4))
                        op("act", "activation", sdb[s2][:], bank(4), AF.Ln, bias=EPS, scale=1.0 / 64,
                           reads=pk(4), writes=[("sdb", s2)])
                        op("act", "activation", sdb[s2][:], sdb[s2][:], AF.Exp, scale=-0.5,
                           reads=[("sdb", s2)], writes=[("sdb", s2)])
                        op("dve", "scalar_tensor_tensor",
                            qT[:, pr, g * 512:(g + 1) * 512], bank(kb_), gq2[:, 0:1], sdb[s2][:], ALU.mult, ALU.mult,
                           reads=pk(kb_) + [("sdb", s2), "gq2"], writes=["qT"])
                    for pr in range(4):
                        kb_ = 6 + (pr % 2)
                        for kc in range(KC):
                            op("pe", "matmul",
                                bank(kb_), WQI[:, kc, pr * 128:(pr + 1) * 128], H[:, kc, :],
                                start=(kc == 0), stop=(kc == KC - 1),
                               reads=["WQI", ("hTo", hs)], writes=pk(kb_))
                        op("act", "activation",
                            qiT[:, pr, g * 512:(g + 1) * 512], bank(kb_), AF.Copy,
                           reads=pk(kb_), writes=["qiT"])
                    for bb in range(4):
                        i = 4 * g + bb
                        for kc in range(KC):
                            op("pe", "matmul",
                                bank(5)[:, 0:8], H[:, kc, bb * 128:(bb + 1) * 128], WWI[:, kc, :],
                                start=(kc == 0), stop=(kc == KC - 1),
                               reads=["WWI", ("hTo", hs)], writes=pk(5))
                        op("dve", "tensor_scalar", wi_s[:, i, :], bank(5)[:, 0:8], 8.0 ** -0.5, None, ALU.mult,
                           reads=pk(5), writes=["wi_s"])

                tilesB = []
                for g in range(NGO):
                    for bb in range(4):
                        i = 4 * g + bb
                        tilesB.append((xo[i, 32:160, :], 128, hTo[g % 2][:, :, bb * 128:(bb + 1) * 128], [("hTo", g % 2)]))
                norm_stream(tilesB, {4 * g + 3: (lambda g=g: groupB(g)) for g in range(NGO)})

            S_.barrier()
            with contextlib.ExitStack() as gC:
                score = sb("score", [128, S], F32, gC)
                maskq = sb("maskq", [128, S], BF16, gC)
                maskT = sb("maskT", [128, NT, 128], BF16, gC)
                rbuf = [sb("rbuf%d" % i, [128, 2, 512], BF16, gC) for i in range(3)]
                wdiag = sb("wdiag", [128, 8, 128], BF16, gC)
                gmax = sb("gmax", [128, 256], F32, gC)
                pen_t = rbuf[0][:].rearrange("p a b -> p (a b)").bitcast(F32)
                bs = sb("bs", [128, 8], F32, gC)
                Wt = sb("Wt", [128, NITER + 1], F32, gC)
                Wt2 = sb("Wt2", [128, NITER + 1], F32, gC)
                vbuf = [sb("vbuf%d" % i, [128, 8 * 65], BF16, gC) for i in range(4)]
                pT = [sb("pT%d" % i, [128, 8, 128], BF16, gC) for i in range(3)]
                att = [sb("att%d" % i, [64, 1024], BF16, gC) for i in range(2)]
                op("dve", "memset", bs[:], 0.0, writes=["bs0", "bs1", "bs2", "mid", "cnt", "tmp", "thr", "cntA"])
                vc_ = {"n": 0}

                def indexer(i):
                    E = 512 * (i + 1)
                    tsl = slice(i * 128, (i + 1) * 128)
                    for h in range(8):
                        op("dve", "tensor_scalar", wdiag[:, h, :], ident_b[:], wi_s[:, i, h:h + 1], None, ALU.mult,
                           reads=["ident_b", "wi_s"], writes=["wdiag"])
                    pairs = [(c, h2) for c in range(i + 1) for h2 in range(4)]

                    def zmm(n):
                        c, h2 = pairs[n]
                        zb = 2 * (n % 3)
                        for hh in range(2):
                            rows = slice(64 * hh, 64 * hh + 64)
                            op("pe", "matmul", bank(zb + hh), qiT[rows, h2, tsl], kiT[rows, c * 512:(c + 1) * 512],
                               start=True, stop=True, reads=["qiT", "kiT"], writes=pk(zb + hh))

                    pend = []
                    pend2 = []

                    def post(c_):
                        sc = score[:, c_ * 512:(c_ + 1) * 512]
                        if c_ == i:
                            op("dve", "scalar_tensor_tensor", sc, kidx[:], c0t[:, i:i + 1], sc, ALU.add, ALU.min,
                               reads=["kidx", "c0t", "score"], writes=["score"])
                        lo_, hi_ = sc[:, 0:256], sc[:, 256:512]
                        if c_ == 0:
                            op("dve", "tensor_tensor", gmax[:], lo_, hi_, ALU.max, reads=["score"], writes=["gmax"])
                        else:
                            op("dve", "tensor_tensor", gmax[:], gmax[:], lo_, ALU.max, reads=["score", "gmax"], writes=["gmax"])
                            op("dve", "tensor_tensor", gmax[:], gmax[:], hi_, ALU.max, reads=["score", "gmax"], writes=["gmax"])

                    zmm(0)
                    if len(pairs) > 1:
                        zmm(1)
                    for n, (c, h2) in enumerate(pairs):
                        sbk = 6 if c % 2 == 0 else 7
                        zb = 2 * (n % 3)
                        rs = n % 3
                        if n + 2 < len(pairs):
                            zmm(n + 2)
                        rflat = rbuf[rs][:].rearrange("p a b -> p (a b)")
                        if n % 2 == 0:
                            op("act", "activation", rflat, bank(zb, 2), AF.Relu, scale=0.125,
                               reads=pk(zb, 2), writes=[("rbuf", rs)])
                        else:
                            op("dve", "tensor_scalar", rflat, bank(zb, 2), 0.0, 0.125, ALU.max, ALU.mult,
                               reads=pk(zb, 2), writes=[("rbuf", rs)])
                        for hh in range(2):
                            h = 2 * h2 + hh
                            op("pe", "matmul", bank(sbk), wdiag[:, h, :], rbuf[rs][:, hh, :],
                               start=(h == 0), stop=(h == 7), reads=["wdiag", ("rbuf", rs)], writes=pk(sbk))
                        if pend2 and n >= pend2[0][0] + 2:
                            post(pend2.pop(0)[1])
                        if pend and n >= pend[0][0] + 2:
                            _, c_, sbk_ = pend.pop(0)
                            op("act", "activation", score[:, c_ * 512:(c_ + 1) * 512], bank(sbk_), AF.Copy,
                               reads=pk(sbk_), writes=["score"])
                            pend2.append((n, c_))
                        if h2 == 3:
                            pend.append((n, c, sbk))
                    for _, c_, sbk_ in pend:
                        op("act", "activation", score[:, c_ * 512:(c_ + 1) * 512], bank(sbk_), AF.Copy,
                           reads=pk(sbk_), writes=["score"])
                        pend2.append((0, c_))
                    for _, c_ in pend2:
                        post(c_)
                    op("dve", "tensor_reduce", bs[:, 0:1], gmax[:], AX.X, ALU.min, reads=["gmax"], writes=["bs0"])
                    op("dve", "tensor_reduce", bs[:, 1:2], gmax[:], AX.X, ALU.max, reads=["gmax"], writes=["bs1"])
                    op("dve", "tensor_scalar", bs[:, 0:1], bs[:, 0:1], -1.0e29, -0.05, ALU.max, ALU.add,
                       reads=["bs0"], writes=["bs0"])
                    op("dve", "tensor_tensor", bs[:, 2:3], bs[:, 1:2], bs[:, 0:1], ALU.subtract,
                       reads=["bs0", "bs1"], writes=["bs2"])
                    op("dve", "tensor_scalar", Wt[:], pow2[:], bs[:, 2:3], None, ALU.mult, reads=["pow2", "bs2"], writes=["Wt"])
                    op("dve", "tensor_scalar", Wt2[:], pow2x2[:], bs[:, 2:3], None, ALU.mult,
                       reads=["pow2x2", "bs2"], writes=["Wt2"])
                    op("dve", "tensor_tensor", bs[:, 3:4], bs[:, 0:1], Wt[:, 0:1], ALU.add, reads=["bs0", "Wt"], writes=["mid"])

                def bisect_iter(i, n):
                    E = 512 * (i + 1)
                    EA = (int(E * ACT_SHARE) // 512) * 512
                    if EA > 0:
                        op("act", "activation", maskq[:, 0:EA], score[:, 0:EA], AF.Sign, bias=bs[:, 3:4], scale=-1.0,
                           accum_out=bs[:, 7:8], reads=["score", "mid"], writes=["maskqA", "cntA"])
                    op("dve", "tensor_scalar", maskq[:, EA:E], score[:, EA:E], bs[:, 3:4], None, ALU.is_gt, ALU.add,
                       accum_out=bs[:, 4:5], reads=["score", "mid"], writes=["maskqD", "cnt"])
                    if EA > 0:
                        op("dve", "tensor_scalar", bs[:, 4:5], bs[:, 7:8], -0.5, bs[:, 4:5], ALU.mult, ALU.add,
                           reads=["cntA", "cnt"], writes=["cnt"])
                    op("dve", "tensor_scalar", bs[:, 5:6], bs[:, 4:5], TOPK - 0.5 - EA / 2.0, Wt2[:, n + 1:n + 2],
                       ALU.is_ge, ALU.mult, reads=["cnt", "Wt2"], writes=["tmp"])
                    op("dve", "tensor_scalar", bs[:, 3:4], bs[:, 5:6], Wt[:, n + 1:n + 2], bs[:, 3:4],
                       ALU.subtract, ALU.add, reads=["tmp", "Wt", "mid"], writes=["mid"])

                def finish_mask(i):
                    E = 512 * (i + 1)
                    nkb = 4 * (i + 1)
                    op("dve", "tensor_tensor", bs[:, 6:7], bs[:, 3:4], Wt[:, NITER:NITER + 1], ALU.subtract,
                       reads=["mid", "Wt"], writes=["thr"])
                    for k8 in range(0, nkb, 8):
                        nn = min(8, nkb - k8)
                        cols = slice(k8 * 128, (k8 + nn) * 128)
                        op("dve", "tensor_scalar", maskq[:, cols], score[:, cols], bs[:, 6:7], None, ALU.is_gt,
                           reads=["score", "thr"], writes=[("mq", k8)])
                    if debug:
                        final_ops.append(dma("sp", dbg[i], bs[:], reads=["thr", "cnt", "mid", "bs0", "bs1", "bs2", "tmp", "cntA"]))
                    for k8 in range(0, nkb, 8):
                        nn = min(8, nkb - k8)
                        tb = 6 if (k8 // 8) % 2 == 0 else 7
                        for kk in range(nn):
                            kb = k8 + kk
                            op("pe", "transpose", bank_bf(tb)[:, kk * 128:(kk + 1) * 128], maskq[:, kb * 128:(kb + 1) * 128],
                               ident_b[:], reads=[("mq", k8), "ident_b"], writes=pk(tb))
                        if (k8 // 8) % 2 == 0:
                            op("act", "activation", maskT[:, k8:k8 + nn, :].rearrange("p a b -> p (a b)"),
                               bank_bf(tb)[:, 0:nn * 128], AF.Copy, reads=pk(tb), writes=["maskT"])
                        else:
                            op("dve", "tensor_copy", maskT[:, k8:k8 + nn, :].rearrange("p a b -> p (a b)"),
                               bank_bf(tb)[:, 0:nn * 128], reads=pk(tb), writes=["maskT"])

                def attention(i):
                    nkb = 4 * (i + 1)
                    tsl = slice(i * 128, (i + 1) * 128)
                    for b2 in (6, 7):
                        op("pe", "matmul", bank(b2)[0:65, :], zeros_b[:, :], KT[:, 0, 0:512], start=True, stop=True,
                           reads=["zeros_b", "KT"], writes=pk(b2))

                    def qk(kb):
                        lb = 2 * (kb % 3)
                        for pr in range(4):
                            for half in range(2):
                                rows = slice(64 * half, 64 * half + 64)
                                op("pe", "matmul", bank(lb + half)[:, pr * 128:(pr + 1) * 128],
                                   KT[rows, pr, kb * 128:(kb + 1) * 128], qT[rows, pr, tsl], start=True, stop=True,
                                   reads=["KT", "qT"], writes=pk(lb + half))

                    def vload(kb):
                        dma("sp", vbuf[kb % 4][:], vscr[kb], reads=[("vscr", kb)], writes=[("vbuf", kb % 4)])

                    vload(0)
                    vload(1)
                    qk(0)
                    if nkb > 1:
                        qk(1)
                    for kb in range(nkb):
                        vs_ = kb % 4
                        if kb + 2 < nkb:
                            vload(kb + 2)
                        lb = 2 * (kb % 3)
                        ps_ = kb % 3
                        op("act", "activation", pT[ps_][:].rearrange("p a b -> p (a b)"), bank(lb, 2), AF.Exp,
                           bias=negshift[:, 0:1], reads=pk(lb, 2) + ["negshift"], writes=[("pT", ps_)])
                        if kb + 2 < nkb:
                            qk(kb + 2)
                        op("pool", "tensor_tensor", pT[ps_][:], pT[ps_][:],
                           maskT[:, kb:kb + 1, :].broadcast_to([128, 8, 128]), ALU.mult,
                           reads=[("pT", ps_), "maskT"], writes=[("pT", ps_)])
                        for h in range(8):
                            b2 = 6 + h // 4
                            hs_ = (h % 2) * 4 + h // 2
                            op("pe", "matmul", bank(b2)[0:65, (h % 4) * 128:(h % 4 + 1) * 128],
                               vbuf[vs_][:, h * 65:(h + 1) * 65], pT[ps_][:, hs_, :], start=False, stop=(kb == nkb - 1),
                               skip_group_check=True, reads=[("vbuf", vs_), ("pT", ps_)], writes=pk(b2))
                        yield
                    as_ = i % 2
                    rdh = wdiag[:].rearrange("p a b -> p (a b)").bitcast(F32)[0:64, :]
                    for hb in range(2):
                        op("act", "activation", pen_t[64:65, :], bank(6 + hb)[64:65, :], AF.Copy,
                           reads=pk(6 + hb), writes=[("rbuf", 0)])
                        op("pe", "matmul", bank(hb)[0:64, :], ones_f[64:65, 0:64], pen_t[64:65, :],
                           start=True, stop=True, reads=["ones_f", ("rbuf", 0)], writes=pk(hb))
                        op("act", "activation", rdh, bank(hb)[0:64, :], AF.Ln, reads=pk(hb), writes=["wdiag"])
                        op("act", "activation", rdh, rdh, AF.Exp, scale=-1.0, reads=["wdiag"], writes=["wdiag"])
                        op("dve", "tensor_tensor", att[as_][:, hb * 512:(hb + 1) * 512], bank(6 + hb)[0:64, :], rdh, ALU.mult,
                           reads=pk(6 + hb) + ["wdiag"], writes=[("att", as_)])
                    o_ = dma("sp", ascr[i], att[as_][:], reads=[("att", as_)], writes=[("ascr", i)])
                    if debug:
                        final_ops.append(o_)
                    yield

                def advance(gen, k):
                    if gen is None:
                        return None
                    for _ in range(k):
                        try:
                            next(gen)
                        except StopIteration:
                            return None
                    return gen

                agen = None
                for i in range(NB):
                    indexer(i)
                    tot = (4 * i + 1) if i > 0 else 0
                    for n in range(NITER):
                        bisect_iter(i, n)
                        agen = advance(agen, (tot * (n + 1)) // NITER - (tot * n) // NITER)
                    agen = advance(agen, 10 ** 6)
                    finish_mask(i)
                    agen = attention(i)
                advance(agen, 10 ** 6)

        S_.barrier()
        def col_load(dst, src_row, n):
            for c in range(n):
                dma("sp", dst[:, c:c + 1], src_row[0:1, c * 128:(c + 1) * 128].rearrange("o p -> p o"),
                    writes=[("col", dst.name)])

        with contextlib.ExitStack() as gDE:
            hT_own = sb("hT_own", [128, KC, TOWN], BF16, gDE)
            convT = sb("convT", [128, 4, TOWN], BF16, gDE)
            with contextlib.ExitStack() as gD:
                alloc_norm(gD, "D")
                WCA = sb("WCA", [128, KC, 512], BF16, gD)
                WCG = sb("WCG", [128, KC, 512], BF16, gD)
                diag = sb("diag", [128, 31, 4, 128], BF16, gD)
                cw_sb = sb("cw_sb", [31, 512], F32, gD)
                cwT = sb("cwT", [128, 4, 31], F32, gD)
                cb_c = sb("cb_c", [128, 4], F32, gD)
                lng_c = sb("lng_c", [128, 4], F32, gD)
                lnb_c = sb("lnb_c", [128, 4], F32, gD)
                hT_gh = [sb("hT_gh%d" % i, [128, KC, 4, 160], BF16, gD) for i in range(2)]
                sgb = [sb("sgb%d" % i, [128, 320], F32, gD) for i in range(2)]
                glu = sb("glu", [128, 4, 4, 160], BF16, gD)
                yb = sb("yb", [128, 4, 512], F32, gD)
                sqy = sb("sqy", [128, 4, 512], F32, gD)
                sdv = sb("sdv", [128, 512], F32, gD)
                dma("pool", WCA[:], w_in_r[:, :, C_CA:C_CA + 512], writes=["WCA"])
                dma("pool", WCG[:], w_in_r[:, :, C_CG:C_CG + 512], writes=["WCG"])
                dma("sp", cw_sb[:], conv_w, writes=["cw_sb"])
                col_load(cb_c, conv_b, 4)
                col_load(lng_c, ln_g, 4)
                col_load(lnb_c, ln_b, 4)
                for cc in range(4):
                    op("pe", "transpose", bank(0)[:, cc * 32:cc * 32 + 31], cw_sb[0:31, cc * 128:(cc + 1) * 128],
                       ident_f[0:31, 0:31], reads=["cw_sb", "ident_f"], writes=pk(0))
                op("dve", "tensor_copy", cwT[:], bank(0)[:, 0:128].rearrange("p (c j) -> p c j", c=4)[:, :, 0:31],
                   reads=pk(0), writes=["cwT"])
                for j in range(31):
                    for cc in range(4):
                        op("dve", "tensor_scalar", diag[:, j, cc, :], ident_b[:], cwT[:, cc, j:j + 1], None, ALU.mult,
                           reads=["ident_b", "cwT"], writes=["diag"])
                def groupD(g):
                    gs = g % 2
                    HG = hT_gh[gs]
                    for bb in range(4):
                        i = 4 * g + bb
                        op("pool", "tensor_copy", hT_own[:, :, i * 128:(i + 1) * 128], HG[:, :, bb, 32:160],
                           reads=[("hT_gh", gs)], writes=["hT_own"])
                    n_ = 0
                    for half in range(2):
                        for cc in range(4):
                            ab, gb = 2 + n_ % 2, 4 + n_ % 2
                            ss_ = n_ % 2
                            n_ += 1
                            for kc in range(KC):
                                op("pe", "matmul", bank(ab)[:, 0:320], WCA[:, kc, cc * 128:(cc + 1) * 128],
                                   HG[:, kc, 2 * half:2 * half + 2, :], start=(kc == 0), stop=(kc == KC - 1),
                                   reads=["WCA", ("hT_gh", gs)], writes=pk(ab))
                            for kc in range(KC):
                                op("pe", "matmul", bank(gb)[:, 0:320], WCG[:, kc, cc * 128:(cc + 1) * 128],
                                   HG[:, kc, 2 * half:2 * half + 2, :], start=(kc == 0), stop=(kc == KC - 1),
                                   reads=["WCG", ("hT_gh", gs)], writes=pk(gb))
                            op("act", "activation", sgb[ss_][:], bank(gb)[:, 0:320], AF.Sigmoid,
                               reads=pk(gb), writes=[("sgb", ss_)])
                            op("dve", "tensor_tensor", glu[:, cc, 2 * half:2 * half + 2, :].rearrange("p a b -> p (a b)"),
                               bank(ab)[:, 0:320], sgb[ss_][:], ALU.mult,
                               reads=pk(ab) + [("sgb", ss_)], writes=["glu"])
                    for cc in range(4):
                        cb_ = 6 + cc % 2
                        for j in range(31):
                            op("pe", "matmul", bank(cb_), diag[:, j, cc, :], glu[:, cc, :, 2 + j:2 + j + 128],
                               start=(j == 0), stop=(j == 30), reads=["diag", "glu"], writes=pk(cb_))
                        op("act", "activation", yb[:, cc, :], bank(cb_), AF.Identity, bias=cb_c[:, cc:cc + 1],
                           reads=pk(cb_) + [("col", "cb_c")], writes=["yb"])
                    for cc in range(4):
                        op("pe", "matmul", bank(2), ones_f[:], yb[:, cc, :], start=(cc == 0), stop=(cc == 3),
                           reads=["ones_f", "yb"], writes=pk(2))
                    for cc in range(4):
                        op("dve", "scalar_tensor_tensor", yb[:, cc, :], bank(2), -1.0 / 512, yb[:, cc, :],
                           ALU.mult, ALU.add, reads=pk(2) + ["yb"], writes=["yb"])
                    op("act", "activation", sqy[:].rearrange("p a b -> p (a b)"), yb[:].rearrange("p a b -> p (a b)"),
                       AF.Square, reads=["yb"], writes=["sqy"])
                    for cc in range(4):
                        op("pe", "matmul", bank(3), ones_f[:], sqy[:, cc, :], start=(cc == 0), stop=(cc == 3),
                           reads=["ones_f", "sqy"], writes=pk(3))
                    op("act", "activation", sdv[:], bank(3), AF.Ln, bias=EPS, scale=1.0 / 512,
                       reads=pk(3), writes=["sdv"])
                    op("act", "activation", sdv[:], sdv[:], AF.Exp, scale=-0.5, reads=["sdv"], writes=["sdv"])
                    for cc in range(4):
                        op("dve", "scalar_tensor_tensor", yb[:, cc, :], yb[:, cc, :], lng_c[:, cc:cc + 1], sdv[:],
                           ALU.mult, ALU.mult, reads=["yb", "sdv", ("col", "lng_c")], writes=["yb"])
                        op("act", "activation", convT[:, cc, g * 512:(g + 1) * 512], yb[:, cc, :], AF.Silu,
                           bias=lnb_c[:, cc:cc + 1], reads=["yb", ("col", "lnb_c")], writes=["convT"])

                tilesD = []
                for g in range(NGO):
                    for bb in range(4):
                        i = 4 * g + bb
                        tilesD.append((xo[i, 0:32, :], 32, hT_gh[g % 2][:, :, bb, 0:32], [("hT_gh", g % 2)]))
                        tilesD.append((xo[i, 32:160, :], 128, hT_gh[g % 2][:, :, bb, 32:160], [("hT_gh", g % 2)]))
                norm_stream(tilesD, {8 * g + 7: (lambda g=g: groupD(g)) for g in range(NGO)})
            S_.barrier()
            with contextlib.ExitStack() as gE:
                WGA = sb("WGA", [128, KC, D], BF16, gE)
                WGC = sb("WGC", [128, KC, D], BF16, gE)
                WAO = sb("WAO", [64, 8, D], BF16, gE)
                WCO = sb("WCO", [128, 4, D], BF16, gE)
                WOUT = sb("WOUT", [128, KC, D], BF16, gE)
                attn_g = [sb("attn_g%d" % i, [64, 8, 4, 128], BF16, gE) for i in range(2)]
                xe = [sb("xe%d" % i, [128, D], F32, gE) for i in range(2)]
                x1T = [sb("x1T%d" % i, [128, KC, 512], F32, gE) for i in range(2)]
                sga = [sb("sga%d" % i, [128, 512], F32, gE) for i in range(2)]
                sgc = [sb("sgc%d" % i, [128, 512], F32, gE) for i in range(2)]
                m1 = [sb("m1_%d" % i, [128, 512], F32, gE) for i in range(2)]
                m2 = [sb("m2_%d" % i, [128, 512], F32, gE) for i in range(2)]
                mT = sb("mT", [128, KC, 512], BF16, gE)
                dma("pool", WGA[:], w_in_r[:, :, C_GA:C_GA + D], writes=["WGA"])
                dma("pool", WGC[:], w_in_r[:, :, C_GC:C_GC + D], writes=["WGC"])
                dma("pool", WAO[:], w_ao.rearrange("(h d) c -> d h c", d=64), writes=["WAO"])
                dma("pool", WCO[:], w_co.rearrange("(cc p) c -> p cc c", p=128), writes=["WCO"])
                dma("pool", WOUT[:], w_out.rearrange("(kc p) c -> p kc c", p=128), writes=["WOUT"])
                xn = 0
                for g in range(NGO):
                    gs = g % 2
                    AG = attn_g[gs]
                    X1 = x1T[gs]
                    for bb in range(4):
                        i = 4 * g + bb
                        dma("sp", AG[:, :, bb, :], ascr[i].rearrange("d (h t) -> d h t", h=8),
                            reads=[("ascr", i)], writes=[("attn_g", gs)])
                        xs_ = xn % 2
                        xn += 1
                        dma("sp", xe[xs_][:], xo[i, 32:160, :], writes=[("xe", xs_)])
                        for q4 in range(2):
                            for k4 in range(4):
                                kc = 4 * q4 + k4
                                op("pe", "transpose", bank(q4)[:, k4 * 128:(k4 + 1) * 128],
                                   xe[xs_][:, kc * 128:(kc + 1) * 128], ident_f[:],
                                   reads=[("xe", xs_), "ident_f"], writes=pk(q4))
                            op("act", "activation", X1[:, 4 * q4:4 * q4 + 4, bb * 128:(bb + 1) * 128],
                               bank(q4).rearrange("p (k t) -> p k t", k=4), AF.Copy,
                               reads=pk(q4), writes=[("x1T", gs)])
                    for dc in range(KC):
                        ds_ = dc % 2
                        dsl = slice(dc * 128, (dc + 1) * 128)
                        for h in range(8):
                            op("pe", "matmul", bank(2), WAO[0:64, h, dsl], AG[0:64, h, :, :],
                               start=(h == 0), stop=(h == 7), reads=["WAO", ("attn_g", gs)], writes=pk(2))
                        for cc in range(4):
                            op("pe", "matmul", bank(3), WCO[:, cc, dsl], convT[:, cc, g * 512:(g + 1) * 512],
                               start=(cc == 0), stop=(cc == 3), reads=["WCO", "convT"], writes=pk(3))
                        for kc in range(KC):
                            op("pe", "matmul", bank(4), WGA[:, kc, dsl], hT_own[:, kc, g * 512:(g + 1) * 512],
                               start=(kc == 0), stop=(kc == KC - 1), reads=["WGA", "hT_own"], writes=pk(4))
                        for kc in range(KC):
                            op("pe", "matmul", bank(5), WGC[:, kc, dsl], hT_own[:, kc, g * 512:(g + 1) * 512],
                               start=(kc == 0), stop=(kc == KC - 1), reads=["WGC", "hT_own"], writes=pk(5))
                        op("act", "activation", sga[ds_][:], bank(4), AF.Sigmoid, reads=pk(4), writes=[("sga", ds_)])
                        op("act", "activation", sgc[ds_][:], bank(5), AF.Sigmoid, reads=pk(5), writes=[("sgc", ds_)])
                        op("dve", "tensor_tensor", m1[ds_][:], bank(2), sga[ds_][:], ALU.mult,
                           reads=pk(2) + [("sga", ds_)], writes=[("m1", ds_)])
                        op("dve", "tensor_tensor", m2[ds_][:], bank(3), sgc[ds_][:], ALU.mult,
                           reads=pk(3) + [("sgc", ds_)], writes=[("m2", ds_)])
                        op("dve", "tensor_tensor", mT[:, dc, :], m1[ds_][:], m2[ds_][:], ALU.add,
                           reads=[("m1", ds_), ("m2", ds_)], writes=["mT"])
                    for dc2 in range(KC):
                        yb_ = 6 + dc2 % 2
                        for dc in range(KC):
                            op("pe", "matmul", bank(yb_), WOUT[:, dc, dc2 * 128:(dc2 + 1) * 128], mT[:, dc, :],
                               start=(dc == 0), stop=(dc == KC - 1), reads=["WOUT", "mT"], writes=pk(yb_))
                        op("dve", "tensor_tensor", X1[:, dc2, :], X1[:, dc2, :], bank(yb_), ALU.add,
                           reads=pk(yb_) + [("x1T", gs)], writes=[("x1T", gs)])
                    dma("sp", x1scr[g], X1[:].rearrange("p k t -> p (k t)"), reads=[("x1T", gs)], writes=[("x1scr", g)])
        S_.barrier()
        with contextlib.ExitStack() as gF:
            NH = TOWN // 1024
            xh = sb("xh", [128, KC, 1024], F32, gF)
            sqx = sb("sqx", [128, KC, 512], BF16, gF)
            sdx = sb("sdx", [128, 512], F32, gF)
            hfT = sb("hfT", [128, KC, 1024], BF16, gF)
            actT = sb("actT", [128, NFF, 1024], BF16, gF)
            gffn_c = sb("gffn_c", [128, KC], F32, gF)
            gple_c = sb("gple_c", [128, KC], F32, gF)
            Wg_s = [sb("Wg_s%d" % i, [128, KC, 128], BF16, gF) for i in range(3)]
            Wu_s = [sb("Wu_s%d" % i, [128, KC, 128], BF16, gF) for i in range(3)]
            Wd_s = [sb("Wd_s%d" % i, [128, NFF, 128], BF16, gF) for i in range(2)]
            WPG = sb("WPG", [128, KC, D], BF16, gF)
            WPP = sb("WPP", [128, 2, D], BF16, gF)
            sgl = [sb("sgl%d" % i, [128, 512], F32, gF) for i in range(2)]
            pob = [sb("pob%d" % i, [128, 256], BF16, gF) for i in range(2)]
            ppT = sb("ppT", [128, 2, 1024], BF16, gF)
            otile = [sb("otile%d" % i, [128, D], F32, gF) for i in range(2)]
            col_load(gffn_c, g_ffn, KC)
            col_load(gple_c, g_ple, KC)
            w_fg_r = w_fg.rearrange("(kc p) f -> p kc f", p=128)
            w_fu_r = w_fu.rearrange("(kc p) f -> p kc f", p=128)
            w_fd_r = w_fd.rearrange("(fc p) d -> p fc d", p=128)

            def rms_feat(gcol, dst):
                for tg in range(2):
                    tsl_ = slice(tg * 512, (tg + 1) * 512)
                    op("act", "activation", sqx[:], xh[:, :, tsl_], AF.Square, reads=["xh"], writes=["sqx"])
                    for kc in range(KC):
                        op("pe", "matmul", bank(0), ones_a[:], sqx[:, kc, :], start=(kc == 0), stop=(kc == KC - 1),
                           reads=["ones_a", "sqx"], writes=pk(0))
                    op("act", "activation", sdx[:], bank(0), AF.Ln, bias=EPS, scale=1.0 / D,
                       reads=pk(0), writes=["sdx"])
                    op("act", "activation", sdx[:], sdx[:], AF.Exp, scale=-0.5, reads=["sdx"], writes=["sdx"])
                    for kc in range(KC):
                        op("dve", "scalar_tensor_tensor", dst[:, kc, tsl_], xh[:, kc, tsl_], gcol[:, kc:kc + 1], sdx[:],
                           ALU.mult, ALU.mult, reads=["xh", "sdx", ("col", gcol.name)], writes=["hfT"])

            wn = 0
            dn = 0
            on = 0
            pn = 0
            for hf in range(NH):
                for gi in range(2):
                    g = 2 * hf + gi
                    dma("sp", xh[:, :, gi * 512:(gi + 1) * 512], x1scr[g].rearrange("p (k t) -> p k t", k=KC),
                        reads=[("x1scr", g)], writes=["xh"])
                rms_feat(gffn_c, hfT)
                n_ = 0
                for ffc in range(NFF):
                    ws = wn % 3
                    wn += 1
                    fsl = slice(ffc * 128, (ffc + 1) * 128)
                    dma("pool", Wg_s[ws][:], w_fg_r[:, :, fsl], writes=[("Wg_s", ws)])
                    dma("pool", Wu_s[ws][:], w_fu_r[:, :, fsl], writes=[("Wu_s", ws)])
                    if hf == 0 and ffc == 2:
                        dma("pool", WPG[:], w_pg.rearrange("(kc p) c -> p kc c", p=128), writes=["WPG"])
                        dma("pool", WPP[:], w_pp.rearrange("(kc p) c -> p kc c", p=128), writes=["WPP"])
                    for tg in range(2):
                        tsl_ = slice(tg * 512, (tg + 1) * 512)
                        gb, ub, ss_ = 1 + n_ % 2, 3 + n_ % 2, n_ % 2
                        n_ += 1
                        for kc in range(KC):
                            op("pe", "matmul", bank(gb), Wg_s[ws][:, kc, :], hfT[:, kc, tsl_],
                               start=(kc == 0), stop=(kc == KC - 1), reads=[("Wg_s", ws), "hfT"], writes=pk(gb))
                        for kc in range(KC):
                            op("pe", "matmul", bank(ub), Wu_s[ws][:, kc, :], hfT[:, kc, tsl_],
                               start=(kc == 0), stop=(kc == KC - 1), reads=[("Wu_s", ws), "hfT"], writes=pk(ub))
                        op("act", "activation", sgl[ss_][:], bank(gb), AF.Silu, reads=pk(gb), writes=[("sgl", ss_)])
                        op("dve", "tensor_tensor", actT[:, ffc, tsl_], sgl[ss_][:], bank(ub), ALU.mult,
                           reads=pk(ub) + [("sgl", ss_)], writes=["actT"])
                n_ = 0
                for dc2 in range(KC):
                    ds_ = dn % 2
                    dn += 1
                    dma("pool", Wd_s[ds_][:], w_fd_r[:, :, dc2 * 128:(dc2 + 1) * 128], writes=[("Wd_s", ds_)])
                    for tg in range(2):
                        tsl_ = slice(tg * 512, (tg + 1) * 512)
                        yb_ = 5 + n_ % 2
                        n_ += 1
                        for ffc in range(NFF):
                            op("pe", "matmul", bank(yb_), Wd_s[ds_][:, ffc, :], actT[:, ffc, tsl_],
                               start=(ffc == 0), stop=(ffc == NFF - 1), reads=[("Wd_s", ds_), "actT"], writes=pk(yb_))
                        op("dve", "tensor_tensor", xh[:, dc2, tsl_], xh[:, dc2, tsl_], bank(yb_), ALU.add,
                           reads=pk(yb_) + ["xh"], writes=["xh"])
                rms_feat(gple_c, hfT)
                for tt in range(8):
                    ps_ = pn % 2
                    pn += 1
                    r0 = hf * 1024 + tt * 128
                    dma("pool", pob[ps_][:], po[r0:r0 + 128, :], writes=[("pob", ps_)])
                    for k2 in range(2):
                        op("pe", "transpose", bank_bf(7)[:, k2 * 128:(k2 + 1) * 128], pob[ps_][:, k2 * 128:(k2 + 1) * 128],
                           ident_b[:], reads=[("pob", ps_), "ident_b"], writes=pk(7))
                    op("act", "activation", ppT[:, :, tt * 128:(tt + 1) * 128],
                       bank_bf(7)[:, 0:256].rearrange("p (k t) -> p k t", k=2), AF.Copy,
                       reads=pk(7), writes=["ppT"])
                n_ = 0
                for dc in range(KC):
                    dsl = slice(dc * 128, (dc + 1) * 128)
                    for tg in range(2):
                        tsl_ = slice(tg * 512, (tg + 1) * 512)
                        gb, ub, ss_ = 1 + n_ % 2, 3 + n_ % 2, n_ % 2
                        n_ += 1
                        for kc in range(KC):
                            op("pe", "matmul", bank(gb), WPG[:, kc, dsl], hfT[:, kc, tsl_],
                               start=(kc == 0), stop=(kc == KC - 1), reads=["WPG", "hfT"], writes=pk(gb))
                        for k2 in range(2):
                            op("pe", "matmul", bank(ub), WPP[:, k2, dsl], ppT[:, k2, tsl_],
                               start=(k2 == 0), stop=(k2 == 1), reads=["WPP", "ppT"], writes=pk(ub))
                        op("act", "activation", sgl[ss_][:], bank(gb), AF.Sigmoid, reads=pk(gb), writes=[("sgl", ss_)])
                        op("dve", "tensor_tensor", sgl[ss_][:], sgl[ss_][:], bank(ub), ALU.mult,
                           reads=pk(ub) + [("sgl", ss_)], writes=[("sgl", ss_)])
                        op("dve", "tensor_tensor", xh[:, dc, tsl_], xh[:, dc, tsl_], sgl[ss_][:], ALU.add,
                           reads=["xh", ("sgl", ss_)], writes=["xh"])
                for tt in range(8):
                    os_ = on % 2
                    on += 1
                    r0 = hf * 1024 + tt * 128
                    for q4 in range(2):
                        ob = 5 + q4
                        for k4 in range(4):
                            kc = 4 * q4 + k4
                            op("pe", "transpose", bank(ob)[:, k4 * 128:(k4 + 1) * 128],
                               xh[:, kc, tt * 128:(tt + 1) * 128], ident_f[:],
                               reads=["xh", "ident_f"], writes=pk(ob))
                        op("act", "activation", otile[os_][:, q4 * 512:(q4 + 1) * 512], bank(ob), AF.Copy,
                           reads=pk(ob), writes=[("otile", os_)])
                    final_ops.append(dma("sp", out_d[r0:r0 + 128, :], otile[os_][:],
                                         reads=[("otile", os_)], writes=[("out", r0)]))
        S_.emit(final_ops)
    return nc


def core_inputs(inp, b, j, S):
    NB = S // 512
    x = np.asarray(inp["x"], dtype=np.float32)[b, :S]
    p = np.asarray(inp["p"], dtype=np.float32)[0, b, :S]
    xo = np.zeros((NB, 160, D), np.float32)
    po = np.zeros((NB * 128, 256), np.float32)
    qpos = np.zeros((128, NB), np.float32)
    for i in range(NB):
        t0 = 128 * (4 * i + j)
        lo = max(t0 - 32, 0)
        xo[i, 160 - (t0 + 128 - lo):, :] = x[lo:t0 + 128]
        po[i * 128:(i + 1) * 128] = p[t0:t0 + 128]
        qpos[:, i] = t0 + np.arange(128)
    m = {"xb": np.ascontiguousarray(x), "xo": xo, "po": po, "qpos": qpos}
    for k in ("g_mix", "w_in", "g_q", "g_k", "conv_b", "conv_ln_g", "conv_ln_b", "w_attn_o", "w_conv_o",
              "w_out", "g_ffn", "w_ffn_gate", "w_ffn_up", "w_ffn_down", "g_ple", "w_ple_gate", "w_ple_proj"):
        a = np.asarray(inp[k], dtype=np.float32)[0]
        if a.ndim == 1:
            a = a[None, :]
        m[k] = np.ascontiguousarray(a)
    m["conv_w"] = np.ascontiguousarray(np.asarray(inp["conv_w"], dtype=np.float32)[0, :, 0, :])
    return m


_NC_CACHE = {}


def kernel(**inputs):
    S = int(np.asarray(inputs["x"]).shape[1])
    B = int(np.asarray(inputs["x"]).shape[0])
    if S not in _NC_CACHE:
        _NC_CACHE[S] = build(S)
    nc = _NC_CACHE[S]
    maps = []
    for c in range(4 * B):
        maps.append(core_inputs(inputs, c // 4, c % 4, S))
    res = run_bass_kernel_spmd(nc, maps, core_ids=list(range(4 * B)))
    NB = S // 512
    out = np.zeros((B, S, D), np.float32)
    for c in range(4 * B):
        b, j = c // 4, c % 4
        o = np.asarray(res.results[c]["out"], dtype=np.float32)
        for i in range(NB):
            t0 = 128 * (4 * i + j)
            out[b, t0:t0 + 128] = o[i * 128:(i + 1) * 128]
    return out
```
